# Optimizing a Trainium2 kernel written in Bass

```python
import jax, jax.numpy as jnp
from jax import lax
import numpy as np

D_MODEL = 2048
BATCH = 2
SEQ = 16384
DEPTH = 1
DEC_BATCH = 32
DEC_SEQ = 64
PAST_LEN = 1024

CHUNK = 64
N_MEM = 256
EPS = 1e-6
CONV_WIDTH = 3
D_CONV = D_MODEL
GLA_HEADS = 4
GLA_DK = D_MODEL // 2 // GLA_HEADS
GLA_DV = D_MODEL // GLA_HEADS
GLA_KEY = GLA_HEADS * GLA_DK
GLA_VAL = GLA_HEADS * GLA_DV
GATE_RANK = 16
GATE_TAU = 16.0
MEM_HEADS = 4
MEM_HD = 128
MEM_W = MEM_HEADS * MEM_HD
N_BRANCH = 3
SPLITS = (D_CONV, D_CONV, D_CONV, D_CONV, GLA_KEY, GLA_KEY, GLA_VAL, GATE_RANK, GLA_VAL, MEM_W, MEM_W, N_BRANCH * D_MODEL)
D_IN = sum(SPLITS)

kernel_name = "hybrid_conv_gla_memxattn_stream_step"


def rmsnorm(x, g):
    xf = x.astype(jnp.float32)
    y = xf * lax.rsqrt(jnp.mean(xf * xf, axis=-1, keepdims=True) + EPS)
    return (y * g.astype(jnp.float32)).astype(x.dtype)


def split_columns(z):
    at = [int(i) for i in np.cumsum(SPLITS)[:-1]]
    return jnp.split(z, at, axis=-1)


def causal_conv(u, prev, w):
    up = jnp.concatenate([prev.astype(u.dtype), u], axis=1)
    y = up[:, :-2] * w[0] + up[:, 1:-1] * w[1] + up[:, 2:] * w[2]
    tail = up[:, -(CONV_WIDTH - 1):]
    return y, tail


def gla_chunked(q, k, v, log_a, s0):
    b, t = q.shape[0], q.shape[1]
    n = -(-t // CHUNK)
    pad = n * CHUNK - t

    def prep(z):
        z = jnp.pad(z, ((0, 0), (0, pad), (0, 0), (0, 0)))
        return z.reshape((b, n, CHUNK) + z.shape[2:]).swapaxes(0, 1)

    qc, kc, vc, ac = prep(q), prep(k), prep(v), prep(log_a)
    mask = jnp.tril(jnp.ones((CHUNK, CHUNK), dtype=bool))

    def step(s, inp):
        qb, kb, vb, ab = inp
        cum = jnp.cumsum(ab, axis=1)
        qf = qb.astype(jnp.float32) * jnp.exp(cum)
        kf = kb.astype(jnp.float32) * jnp.exp(-cum)
        kend = kb.astype(jnp.float32) * jnp.exp(cum[:, -1:] - cum)
        vf = vb.astype(jnp.float32)
        att = jnp.where(mask, jnp.einsum('blhd,bmhd->bhlm', qf, kf), 0.0)
        o = jnp.einsum('bhlm,bmhv->blhv', att, vf) + jnp.einsum('blhd,bhdv->blhv', qf, s)
        s_new = jnp.exp(cum[:, -1])[..., None] * s + jnp.einsum('blhd,blhv->bhdv', kend, vf)
        return s_new, o

    s_fin, o = lax.scan(step, s0.astype(jnp.float32), (qc, kc, vc, ac))
    o = o.swapaxes(0, 1).reshape((b, n * CHUNK) + o.shape[3:])[:, :t]
    return o, s_fin


def memory_kv(mem, g_mem, w_mem_k, w_mem_v):
    m = rmsnorm(mem, g_mem)
    bsz = mem.shape[0]
    k = (m @ w_mem_k).reshape(bsz, N_MEM, MEM_HEADS, MEM_HD)
    v = (m @ w_mem_v).reshape(bsz, N_MEM, MEM_HEADS, MEM_HD)
    return k, v


def trunk_layer(x, conv_prev, s_prev, mem_k, mem_v, g_pre, g_post, w_in, conv_w,
                w_gate_up, b_gate, gla_norm, w_out_a, w_out_b, w_out_x, w_final):
    bsz, t, _ = x.shape
    h = rmsnorm(x, g_pre)
    z = h @ w_in
    (a_in, a_b, a_c, a_gate, gq, gk, gv, g_lr, g_gate, xq, x_gate, merge) = split_columns(z)

    cv, conv_tail = causal_conv(a_c * a_in, conv_prev, conv_w)
    y_a = (a_b * cv * jax.nn.silu(a_gate)) @ w_out_a

    q = gq.reshape(bsz, t, GLA_HEADS, GLA_DK) * (GLA_DK ** -0.5)
    k = gk.reshape(bsz, t, GLA_HEADS, GLA_DK)
    v = gv.reshape(bsz, t, GLA_HEADS, GLA_DV)
    log_a = jax.nn.log_sigmoid((g_lr @ w_gate_up + b_gate).astype(jnp.float32)) / GATE_TAU
    log_a = log_a.reshape(bsz, t, GLA_HEADS, GLA_DK)
    o, s_new = gla_chunked(q, k, v, log_a, s_prev)
    o = rmsnorm(o, gla_norm).astype(x.dtype).reshape(bsz, t, GLA_VAL)
    y_b = (o * jax.nn.silu(g_gate)) @ w_out_b

    qx = xq.reshape(bsz, t, MEM_HEADS, MEM_HD)
    sc = jnp.einsum('bthd,bmhd->bhtm', qx, mem_k.astype(x.dtype)).astype(jnp.float32) * (MEM_HD ** -0.5)
    p = jax.nn.softmax(sc, axis=-1).astype(x.dtype)
    ox = jnp.einsum('bhtm,bmhd->bthd', p, mem_v.astype(x.dtype)).reshape(bsz, t, MEM_W)
    y_x = (ox * jax.nn.silu(x_gate)) @ w_out_x

    m_a, m_b, m_x = jnp.split(jax.nn.sigmoid(merge), N_BRANCH, axis=-1)
    y = (m_a * y_a + m_b * y_b + m_x * y_x) @ w_final
    out = x + rmsnorm(y, g_post)
    return out, conv_tail, s_new.astype(x.dtype)


def setup_inputs(seed: int = 0) -> dict:
    key = jax.random.key(seed)
    ks = jax.random.split(key, 24)
    f32 = jnp.float32
    nrm = lambda k, shape, s: jax.random.normal(k, shape, f32) * s
    D = D_MODEL
    return {
        "x_prompt": nrm(ks[0], (BATCH, SEQ, D), 1.0),
        "x_sample": nrm(ks[1], (DEC_BATCH, DEC_SEQ, D), 1.0),
        "cache_conv": nrm(ks[2], (DEPTH, DEC_BATCH, CONV_WIDTH - 1, D_CONV), 0.5),
        "state_gla": nrm(ks[3], (DEPTH, DEC_BATCH, GLA_HEADS, GLA_DK, GLA_DV), 0.1),
        "cache_mem_k": nrm(ks[4], (DEPTH, DEC_BATCH, N_MEM, MEM_HEADS, MEM_HD), 1.0),
        "cache_mem_v": nrm(ks[5], (DEPTH, DEC_BATCH, N_MEM, MEM_HEADS, MEM_HD), 1.0),
        "mem_prompt": nrm(ks[6], (BATCH, N_MEM, D), 1.0),
        "g_pre": 1.0 + nrm(ks[7], (DEPTH, D), 0.02),
        "g_post": 1.0 + nrm(ks[8], (DEPTH, D), 0.02),
        "g_mem": 1.0 + nrm(ks[9], (DEPTH, D), 0.02),
        "w_in": nrm(ks[10], (DEPTH, D, D_IN), D ** -0.5),
        "conv_w": nrm(ks[11], (DEPTH, CONV_WIDTH, D_CONV), CONV_WIDTH ** -0.5),
        "w_gate_up": nrm(ks[12], (DEPTH, GATE_RANK, GLA_KEY), GATE_RANK ** -0.5),
        "b_gate": nrm(ks[13], (DEPTH, GLA_KEY), 0.1),
        "gla_norm": 1.0 + nrm(ks[14], (DEPTH, GLA_DV), 0.02),
        "w_mem_k": nrm(ks[15], (DEPTH, D, MEM_W), D ** -0.5),
        "w_mem_v": nrm(ks[16], (DEPTH, D, MEM_W), D ** -0.5),
        "w_out_a": nrm(ks[17], (DEPTH, D_CONV, D), D_CONV ** -0.5),
        "w_out_b": nrm(ks[18], (DEPTH, GLA_VAL, D), GLA_VAL ** -0.5),
        "w_out_x": nrm(ks[19], (DEPTH, MEM_W, D), MEM_W ** -0.5),
        "w_final": nrm(ks[20], (DEPTH, D, D), D ** -0.5),
    }


def reference(x_prompt, x_sample, cache_conv, state_gla, cache_mem_k, cache_mem_v, mem_prompt,
              g_pre, g_post, g_mem, w_in, conv_w, w_gate_up, b_gate, gla_norm,
              w_mem_k, w_mem_v, w_out_a, w_out_b, w_out_x, w_final):
    xp, xs = x_prompt, x_sample
    conv_p, gla_p, memk_p, memv_p, conv_s, gla_s = [], [], [], [], [], []
    for l in range(DEPTH):
        lw = (g_pre[l], g_post[l], w_in[l], conv_w[l], w_gate_up[l], b_gate[l], gla_norm[l],
              w_out_a[l], w_out_b[l], w_out_x[l], w_final[l])
        mk, mv = memory_kv(mem_prompt, g_mem[l], w_mem_k[l], w_mem_v[l])
        zero_conv = jnp.zeros((xp.shape[0], CONV_WIDTH - 1, D_CONV), xp.dtype)
        zero_s = jnp.zeros((xp.shape[0], GLA_HEADS, GLA_DK, GLA_DV), xp.dtype)
        xp, ct, st = trunk_layer(xp, zero_conv, zero_s, mk, mv, *lw)
        conv_p.append(ct); gla_p.append(st); memk_p.append(mk); memv_p.append(mv)
        xs, ct2, st2 = trunk_layer(xs, cache_conv[l], state_gla[l], cache_mem_k[l], cache_mem_v[l], *lw)
        conv_s.append(ct2); gla_s.append(st2)
    y_prompt, y_sample = xp, xs
    return (y_prompt, y_sample, jnp.stack(conv_p), jnp.stack(gla_p), jnp.stack(memk_p), jnp.stack(memv_p), jnp.stack(conv_s), jnp.stack(gla_s))
```

```python
import numpy as np
from contextlib import ExitStack

import concourse.bass as bass
import concourse.mybir as mybir
from concourse.bass_utils import run_bass_kernel_spmd

F32 = mybir.dt.float32
BF16 = mybir.dt.bfloat16
AF = mybir.ActivationFunctionType
ALU = mybir.AluOpType
AX = mybir.AxisListType
DSZ = {F32: 4, BF16: 2}

D = 2048
KC = 16
NCORE = 8
TT = 256
NS = TT // 128
NHALO = 2
EPS = 1e-6
NH = 4
DK = 256
DV = 512
NMEM = 256
XH = 4
GATE_TAU = 16.0

C_IDENT = 0
C_U128 = 128
C_UB64 = 256
C_M128 = 384
C_MB64 = 512
C_ONES = 640
C_GPRE = 768
C_GPOST = 784
C_GMEM = 800
C_CONVW = 816
C_GNORM = 864
C_CMAT = 1376
C_MASK = 1440
C_EPS = 1448
C_ONE = 1449
CST_N = 1456


def _iv(ap):
    t = ap.tensor
    name = t.name
    a = ap.ap
    off = int(ap.offset)
    dsz = DSZ.get(ap.dtype, 4)
    tn = type(t).__name__
    if "DRam" in tn:
        if not name.startswith("wb_"):
            return None
        span = sum(int(s) * (int(c) - 1) for s, c in a) + 1
        return name, 0, 1, off * dsz, (off + span) * dsz
    pstride, npart = int(a[0][0]), int(a[0][1])
    p0 = off // pstride
    f0 = off % pstride
    span = sum(int(s) * (int(c) - 1) for s, c in a[1:]) + 1
    return name, p0, p0 + npart, f0 * dsz, (f0 + span) * dsz


def _is_psum(ap):
    return "PSum" in type(ap.tensor).__name__


class Prog:
    COMPUTE = ("pe", "act", "dve", "pool")
    QUEUES = ("sp", "pool", "act")
    NDSEM = 16

    def __init__(self, nc):
        self.nc = nc
        self.ops = []
        self.recs = {}
        self.psrec = {}
        self.dma_count = {q: 0 for q in self.QUEUES}
        self.dma_ops = {q: [] for q in self.QUEUES}
        self.out_dmas = []

    def _deps(self, eng, reads, writes, is_dma):
        deps = set()
        for ap in reads:
            iv = _iv(ap)
            if iv is None:
                continue
            name, p0, p1, b0, b1 = iv
            rec = self.recs.setdefault(name, {"w": [], "r": []})
            for (q0, q1, c0, c1, oid) in rec["w"]:
                if q0 < p1 and p0 < q1 and c0 < b1 and b0 < c1:
                    deps.add(oid)
        for ap in writes:
            iv = _iv(ap)
            if iv is None:
                continue
            name, p0, p1, b0, b1 = iv
            rec = self.recs.setdefault(name, {"w": [], "r": []})
            for (q0, q1, c0, c1, oid) in rec["w"]:
                if q0 < p1 and p0 < q1 and c0 < b1 and b0 < c1:
                    deps.add(oid)
            for (q0, q1, c0, c1, oid, _e) in rec["r"]:
                if q0 < p1 and p0 < q1 and c0 < b1 and b0 < c1:
                    deps.add(oid)
        return deps

    def _record(self, oid, eng, reads, writes, is_dma):
        for ap in writes:
            iv = _iv(ap)
            if iv is None:
                continue
            name, p0, p1, b0, b1 = iv
            rec = self.recs[name]
            rec["w"] = [r for r in rec["w"]
                        if not (p0 <= r[0] and r[1] <= p1 and b0 <= r[2] and r[3] <= b1)]
            rec["r"] = [r for r in rec["r"]
                        if not (p0 <= r[0] and r[1] <= p1 and b0 <= r[2] and r[3] <= b1)]
            rec["w"].append((p0, p1, b0, b1, oid))
        for ap in reads:
            iv = _iv(ap)
            if iv is None:
                continue
            name, p0, p1, b0, b1 = iv
            rec = self.recs[name]
            if not is_dma:
                rec["r"] = [r for r in rec["r"]
                            if not (r[5] == eng and r[0] == p0 and r[1] == p1 and r[2] == b0 and r[3] == b1)]
            rec["r"].append((p0, p1, b0, b1, oid, eng if not is_dma else "dma"))

    def op(self, eng, fn, reads=(), writes=(), dma=False, out_dma=False, extra_deps=()):
        reads = [r for r in reads if r is not None and not isinstance(r, (int, float))]
        writes = [w for w in writes if w is not None]
        oid = len(self.ops)
        ps_reads = [r for r in reads if _is_psum(r)]
        ps_writes = [w for w in writes if _is_psum(w)]
        reads = [r for r in reads if not _is_psum(r)]
        writes = [w for w in writes if not _is_psum(w)]
        deps = self._deps(eng, reads, writes, dma)
        deps.update(extra_deps)
        for aps, isw in ((ps_reads, False), (ps_writes, True)):
            for ap in aps:
                name, p0, p1, b0, b1 = _iv(ap)
                lst = self.psrec.setdefault(name, [])
                keep = []
                for (q0, q1, c0, c1, o2, e2, w2) in lst:
                    if e2 != eng:
                        deps.add(o2)
                    elif (isw or w2) and q0 < p1 and p0 < q1 and c0 < b1 and b0 < c1:
                        deps.add(o2)
                    elif not (q0 == p0 and q1 == p1 and c0 == b0 and c1 == b1 and w2 == isw):
                        keep.append((q0, q1, c0, c1, o2, e2, w2))
                keep.append((p0, p1, b0, b1, oid, eng, isw))
                self.psrec[name] = keep
        if not dma and eng == "pe":
            deps = {d for d in deps if not (self.ops[d]["eng"] == "pe" and not self.ops[d]["dma"])}
        latest = {}
        for d_ in list(deps):
            od = self.ops[d_]
            if not od["dma"]:
                e2 = od["eng"]
                if e2 in latest:
                    if d_ > latest[e2]:
                        deps.discard(latest[e2])
                        latest[e2] = d_
                    else:
                        deps.discard(d_)
                else:
                    latest[e2] = d_
        o = {"eng": eng, "fn": fn, "deps": deps, "dma": dma}
        if dma:
            k = self.dma_count[eng]
            self.dma_count[eng] += 1
            o["dsem"] = k % self.NDSEM
            o["dval"] = 16 * (k // self.NDSEM + 1)
            if k >= self.NDSEM:
                deps.add(self.dma_ops[eng][k - self.NDSEM])
            self.dma_ops[eng].append(oid)
            if out_dma:
                self.out_dmas.append(oid)
        self.ops.append(o)
        self._record(oid, eng, reads, writes, dma)
        return oid

    def mm(self, out, lhsT, rhs, start=True, stop=True):
        return self.op("pe", lambda e: e.matmul(out, lhsT, rhs, start=start, stop=stop),
                       reads=[lhsT, rhs], writes=[out])

    def tr(self, out, in_, ident):
        return self.op("pe", lambda e: e.transpose(out, in_, ident), reads=[in_, ident], writes=[out])

    def act(self, out, in_, func, bias=0.0, scale=1.0, accum=None):
        kw = {}
        if accum is not None:
            kw["accum_out"] = accum
        return self.op("act", lambda e: e.activation(out=out, in_=in_, func=func, bias=bias, scale=scale, **kw),
                       reads=[in_, bias, scale], writes=[out, accum])

    def tt(self, eng, out, a, b, op):
        return self.op(eng, lambda e: e.tensor_tensor(out, a, b, op), reads=[a, b], writes=[out])

    def ts(self, eng, out, a, s1, s2, op0, op1=None):
        if op1 is None:
            return self.op(eng, lambda e: e.tensor_scalar(out, a, s1, None, op0), reads=[a, s1], writes=[out])
        return self.op(eng, lambda e: e.tensor_scalar(out, a, s1, s2, op0, op1), reads=[a, s1, s2], writes=[out])

    def stt(self, eng, out, in0, scalar, in1, op0, op1):
        return self.op(eng, lambda e: e.scalar_tensor_tensor(out, in0, scalar, in1, op0, op1),
                       reads=[in0, scalar, in1], writes=[out])

    def copy(self, eng, out, in_):
        if eng == "act":
            return self.op("act", lambda e: e.copy(out, in_), reads=[in_], writes=[out])
        return self.op(eng, lambda e: e.tensor_copy(out, in_), reads=[in_], writes=[out])

    def memset(self, eng, ap, val):
        return self.op(eng, lambda e: e.memset(ap, val), writes=[ap])

    def dma(self, q, out, in_, out_dma=False):
        return self.op(q, lambda e: e.dma_start(out=out, in_=in_), reads=[in_], writes=[out], dma=True,
                       out_dma=out_dma)

    def emit(self, stack):
        nc = self.nc
        ops = self.ops
        engs = ("pe", "act", "dve", "pool", "sp")
        ops.append({"eng": "sp", "fn": None, "deps": set(self.out_dmas), "dma": False})
        targets = set()
        for o in ops:
            targets.update(o["deps"])
        csem = {e: stack.enter_context(nc.semaphore("c_" + e)) for e in engs}
        dsem = {q: [stack.enter_context(nc.semaphore("d_%s%d" % (q, i))) for i in range(self.NDSEM)]
                for q in self.QUEUES}
        cnt = {e: 0 for e in engs}
        for i, o in enumerate(ops):
            if o["dma"]:
                o["ev"] = (("d", o["eng"], o["dsem"]), o["dval"])
            elif i in targets:
                cnt[o["eng"]] += 1
                o["ev"] = (("c", o["eng"]), cnt[o["eng"]])
            else:
                o["ev"] = None
        per_eng = {e: [] for e in engs}
        for i, o in enumerate(ops):
            per_eng[o["eng"]].append(i)
        handles = {"pe": "tensor", "act": "scalar", "dve": "vector", "pool": "gpsimd", "sp": "sync"}

        def semof(key):
            return csem[key[1]] if key[0] == "c" else dsem[key[1]][key[2]]

        def body(ename):
            def f(e):
                waited = {}
                for i in per_eng[ename]:
                    o = ops[i]
                    need = {}
                    for d in o["deps"]:
                        k, v = ops[d]["ev"]
                        if waited.get(k, 0) >= v:
                            continue
                        if need.get(k, 0) < v:
                            need[k] = v
                    for k, v in need.items():
                        e.wait_ge(semof(k), v)
                        waited[k] = v
                    if o["fn"] is None:
                        continue
                    ins = o["fn"](e)
                    if o["dma"]:
                        ins.then_inc(dsem[o["eng"]][o["dsem"]], 16)
                    elif o["ev"] is not None:
                        ins.then_inc(csem[ename], 1)
            return f

        block = stack.enter_context(nc.Block())
        for ename in engs:
            getattr(block, handles[ename])(body(ename))


def build_program(SEG, phase1=True, n_prompt_tiles=None, upto=99):
    NTP = SEG // TT if n_prompt_tiles is None else n_prompt_tiles
    nc = bass.Bass("TRN2", target_bir_lowering=False)
    P = Prog(nc)
    st = ExitStack()

    def din(name, shape, dt=F32):
        return nc.dram_tensor(name, list(shape), dt, kind="ExternalInput").ap()

    def dout(name, shape, dt=F32):
        return nc.dram_tensor(name, list(shape), dt, kind="ExternalOutput").ap()

    def sb(name, shape, dt=F32):
        return st.enter_context(nc.sbuf_tensor(name, list(shape), dt))

    xpT = din("xpT", [D, SEG])
    xsT = din("xsT", [D, TT + NHALO])
    xpre = din("xpre", [D, 3 * SEG]) if phase1 else None
    memT = din("memT", [D, NMEM])
    cst_d = din("cst", [128, CST_N])
    wgb_d = din("wgb", [32, 1024])
    win_blk = din("win_blk", [42, 128, KC, 512])
    win_glr = din("win_glr", [128, KC, 16])
    wouta = din("wouta", [4, 128, KC, 512])
    woutb = din("woutb", [4, 128, KC, 512])
    woutx = din("woutx", [4, 128, 4, 512])
    wfin = din("wfin", [4, 128, KC, 512])
    wmk = din("wmk", [128, KC, 512])
    wmv = din("wmv", [128, KC, 512])
    convc_d = din("convc", [128, 4, KC, 2])
    sg_d = din("sg", [4, NH, DK, DV])
    mkc_d = din("mkc", [4, 128, XH, NMEM])
    mvc_d = din("mvc", [4, NMEM, 512])

    ypT = dout("ypT", [D, SEG])
    ysT = dout("ysT", [D, TT])
    convp_o = dout("convp", [128, KC, 2])
    convs_o = dout("convs", [128, 4, KC, 2])
    glap_o = dout("glap", [NH, DK, DV])
    glas_o = dout("glas", [4, NH, DK, DV])
    mkT_o = dout("mkT", [512, NMEM])
    mv_o = dout("mvo", [NMEM, 512])

    NTM = TT + NHALO
    TTP = 512
    NSP = TTP // 128
    cst = sb("cst_s", [128, CST_N])
    wgb = sb("wgb_s", [32, 1024], BF16)
    identb = sb("identb", [128, 128], BF16)
    onesb = sb("onesb", [128, 128], BF16)
    sq = [sb("sq%d" % i, [128, NTM], BF16) for i in range(2)]
    rstd_b = sb("rstd_b", [128, NTM])
    hT = sb("hT", [128, KC, NTM], BF16)
    NW = 3
    wsl = [sb("wsl%d" % i, [128, KC, 512], BF16) for i in range(NW)]
    uprev = sb("uprev", [128, KC, 2])
    convc = sb("convc_s", [128, 4, KC, 2])
    convs_st = sb("convs_st", [128, 4, KC, 2])
    glr = sb("glr", [32, NTM], BF16)
    ssq = sb("ssq", [128, 8])
    rs = sb("rs", [128, 8])
    Sst = sb("Sst", [128, NH * 2, DV])
    Sbf = sb("Sbf", [128, NH * 2, DV], BF16)
    KTp = sb("KTp", [128, XH, NMEM], BF16)
    Vp = sb("Vp", [128, 2, 512], BF16)
    mx = sb("mx", [128, 8])
    ssum = sb("ssum", [128, 8])

    ARENA_B = 106 * 1024
    arena = sb("arena", [128, ARENA_B // 4])

    def av(ptr, shape, dt=F32):
        n = 1
        for d_ in shape[1:]:
            n *= d_
        nb = (n * DSZ[dt] + 31) // 32 * 32
        a = arena[:, ptr[0] // 4:(ptr[0] + nb) // 4]
        ptr[0] += nb
        assert ptr[0] <= ARENA_B, ptr[0]
        if dt == BF16:
            a = a.bitcast(BF16)
        a = a[:, 0:n]
        if len(shape) == 3:
            a = a.rearrange("p (a b) -> p a b", a=shape[1])
        elif len(shape) == 4:
            a = a.rearrange("p (a b c) -> p a b c", a=shape[1], b=shape[2])
        return a

    mp = [0]
    big = av(mp, [128, KC, TT])
    xin0 = av(mp, [128, KC, NTM])
    zA = av(mp, [128, KC, TT], BF16)
    zB = zA
    zX = av(mp, [128, 4, TT], BF16)
    mixf = big
    mixb = zA
    yT = big
    qfz = [av(mp, [128, 2, TT], BF16) for i in range(2)]
    sbase = mp[0]
    S0f = [av(mp, [128, 2, 2, DV]) for i in range(1)]
    S0b = [av(mp, [128, 2, 2, DV], BF16) for i in range(1)]
    sbig = av(mp, [128, NH * 2, DV])
    Snew = [sbig[:, 4 * i:4 * i + 4, :].rearrange("p (j d) v -> p j d v", j=2) for i in range(2)]
    mkst = sbig[:, 0:2, :].rearrange("p a (h m) -> p (a h) m", h=2)
    mvst = sbig[:, 2:4, :]
    xin1 = av([sbase], [128, KC, TT])
    xinb = [xin0, xin1]
    gb = mp[0]
    ex = [av(mp, [128, NS, DK]) for i in range(2)]
    nla = [av(mp, [128, NS, DK]) for i in range(2)]
    Ep = [av(mp, [128, 2, TT]) for i in range(2)]
    En = [av(mp, [128, 2, TT]) for i in range(2)]
    Ee = [av(mp, [128, 2, TT]) for i in range(2)]
    cl = [av(mp, [128, 2, 4]) for i in range(2)]
    qf = av(mp, [128, 2, TT], BF16)
    kf = av(mp, [128, 2, TT], BF16)
    ke = av(mp, [128, 2, TT], BF16)
    keTM = av(mp, [128, NS, DK], BF16)
    vTM = av(mp, [128, NS, DV], BF16)
    gsil = av(mp, [128, NS, DV])
    attm = [av(mp, [128, 128], BF16) for i in range(2)]
    zb = [av(mp, [128, DV], BF16) for i in range(2)]
    top = mp[0]
    mp = [gb]
    t1 = [av(mp, [128, NTM]) for i in range(2)]
    ubuf = [av(mp, [128, 4 * 66]) for i in range(2)]
    cv = [av(mp, [128, TT]) for i in range(2)]
    sgA = [av(mp, [128, NTM]) for i in range(2)]
    bsA = [av(mp, [128, TT]) for i in range(2)]
    top = max(top, mp[0])
    mp = [gb + 6 * 1024 + 4 * 1024]
    qxT = av(mp, [128, XH, TT], BF16)
    sgx = av(mp, [128, XH, TT], BF16)
    KTs = [av(mp, [128, XH, NMEM], BF16) for i in range(1)]
    Vs = [av(mp, [128, 2, 512], BF16) for i in range(1)]
    pe_ = [av(mp, [128, NMEM]) for i in range(4)]
    pn = [av(mp, [128, NMEM], BF16) for i in range(4)]
    pT = [av(mp, [128, 2, 128], BF16) for i in range(4)]
    top = max(top, mp[0])
    mp = [gb]
    sg3 = [av(mp, [128, TT]) for i in range(4)]
    t3 = [av(mp, [128, TT]) for i in range(2)]
    outT = [av(mp, [128, TT]) for i in range(4)]
    top = max(top, mp[0])
    mp = [top]

    pp_ = [0]
    xinP = av(pp_, [128, KC, TTP])
    hTPb = [av(pp_, [128, KC, TTP], BF16) for i in range(2)]
    sqP = [av(pp_, [128, TTP], BF16) for i in range(2)]
    rstdP = av(pp_, [128, TTP])
    glrP = av(pp_, [32, TTP], BF16)
    exP = [av(pp_, [128, NSP, DK]) for i in range(2)]
    nlaP = [av(pp_, [128, NSP, DK]) for i in range(2)]
    EeP = [av(pp_, [128, 2, TTP]) for i in range(2)]
    clP = [av(pp_, [128, 2, NSP]) for i in range(2)]
    decP = [av(pp_, [128, 2, NSP]) for i in range(2)]
    TsP = [av(pp_, [128, 2, NSP]) for i in range(2)]
    keP = av(pp_, [128, 2, TTP], BF16)
    keTMP = av(pp_, [128, NSP, DK], BF16)
    vTMP = av(pp_, [128, NSP, DV], BF16)
    print("arena main %d prefix %d of %d; sbuf left %d" % (mp[0], pp_[0], ARENA_B, nc.sbuf_bytes_remaining))

    banks = [st.enter_context(nc.psum_tensor("bank%d" % i, [128, 512], F32)) for i in range(8)]
    bctr = [0]

    ROT = (0, 1, 2, 3, 4, 5, 7)

    def bank():
        b = banks[ROT[bctr[0] % len(ROT)]]
        bctr[0] += 1
        return b

    LB = banks[6]

    wctr = [0]

    scratch = {}
    cached = set()

    def wload(src_full, idx=None, kc=KC, ncol=512, c0=0, nc_=None):
        name = src_full.tensor.name
        if name not in scratch:
            scratch[name] = nc.dram_tensor("wb_" + name, list(src_full.shape), BF16, kind="Internal").ap()
        sc = scratch[name]
        src = src_full if idx is None else src_full[idx]
        scv = sc if idx is None else sc[idx]
        s = wsl[wctr[0] % NW]
        wctr[0] += 1
        if (name, idx) not in cached:
            cached.add((name, idx))
            P.dma("pool", s[:, 0:kc, 0:ncol], src)
            P.dma("pool", scv, s[:, 0:kc, 0:ncol])
            return s, c0
        if nc_ is None:
            P.dma("sp", s[:, 0:kc, 0:ncol], scv)
            return s, c0
        P.dma("sp", s[:, 0:kc, 0:nc_], scv[:, :, c0:c0 + nc_])
        return s, 0

    def precast(src_full, idx):
        name = src_full.tensor.name
        if name not in scratch:
            scratch[name] = nc.dram_tensor("wb_" + name, list(src_full.shape), BF16, kind="Internal").ap()
        if (name, idx) in cached:
            return
        cached.add((name, idx))
        P.dma("pool", scratch[name][idx], src_full[idx])

    rr = {"a": 0}

    def alt(lst, key):
        rr[key] = rr.get(key, 0) + 1
        return lst[rr[key] % len(lst)]

    c_ = lambda o, n: cst[:, o:o + n]
    ident_f = c_(C_IDENT, 128)
    U128 = c_(C_U128, 128)
    UB64 = c_(C_UB64, 128)
    M128 = c_(C_M128, 128)
    MB64 = c_(C_MB64, 128)
    onesD = c_(C_ONES, 128)
    gnorm_b = c_(C_GNORM, 512)

    P.dma("sp", cst[:], cst_d[:, :])
    P.dma("pool", wgb[:], wgb_d[:, :])
    P.dma("sp", convc[:], convc_d[:, :, :, :])
    P.copy("dve", identb[:], ident_f)
    P.copy("dve", onesb[:], onesD)
    P.memset("dve", glr[:], 1.0)
    P.memset("dve", Sst[:], 0.0)
    P.memset("dve", Sbf[:], 0.0)
    P.memset("dve", uprev[:], 0.0)

    def xload(xT_ap, NT, xi=0):
        P.dma("pool", xinb[xi][:, :, 0:NT], xT_ap.rearrange("(kc p) n -> p kc n", p=128))

    def stage0(xT_ap, NT, gcol, load=True, xi=0):
        xin = xinb[xi]
        if load:
            xload(xT_ap, NT, xi)
        ps = LB
        for kc in range(KC):
            s = alt(sq, "sq")
            P.act(s[:, 0:NT], xin[:, kc, 0:NT], AF.Square)
            P.mm(ps[:, 0:NT], onesb[:], s[:, 0:NT], start=(kc == 0), stop=(kc == KC - 1))
        rsqrt(rstd_b[:, 0:NT], ps[:, 0:NT])
        for kc in range(KC):
            P.stt("dve", hT[:, kc, 0:NT], xin[:, kc, 0:NT], cst[:, gcol + kc:gcol + kc + 1],
                  rstd_b[:, 0:NT], ALU.mult, ALU.mult)

    def rsqrt(out, in_):
        P.act(out, in_, AF.Ln, bias=cst[0:out.shape[0], C_EPS:C_EPS + 1])
        P.act(out, out, AF.Exp, scale=-0.5)

    def fm(ps_ap, w, col0, rhs, NT, kcn=KC, m=128):
        for kc in range(kcn):
            P.mm(ps_ap, w[:, kc, col0:col0 + m], rhs[:, kc, 0:NT], start=(kc == 0), stop=(kc == kcn - 1))

    def tm(ps_ap, w, lhs, sub, ncol=512, kcn=KC):
        for kc in range(kcn):
            P.mm(ps_ap, lhs[:, kc, sub * 128:(sub + 1) * 128], w[:, kc, 0:ncol],
                 start=(kc == 0), stop=(kc == kcn - 1))

    def branch_a(NT, sample):
        for c in range(KC):
            w, _ = wload(win_blk, c)
            p1, p2, p3, p4 = bank(), bank(), bank(), bank()
            fm(p1[:, 0:NT], w, 0, hT, NT)
            fm(p2[:, 0:NT], w, 128, hT, NT)
            fm(p3[:, 0:NT], w, 256, hT, NT)
            fm(p4[:, 0:NT], w, 384, hT, NT)
            a1 = alt(t1, "t1")
            ub = alt(ubuf, "ub")
            cvb = alt(cv, "cv")
            sgb = alt(sgA, "sgA")
            bsb = alt(bsA, "bsA")
            P.copy("act", a1[:, 0:NT], p1[:, 0:NT])
            if sample:
                u3 = ub[:, 0:264].rearrange("p (s l) -> p s l", s=4)
                P.copy("dve", u3[:, :, 0:2], convc[:, :, c, :])
                P.tt("dve", u3[:, :, 2:66], p2[:, 0:TT].rearrange("p (s l) -> p s l", s=4),
                     a1[:, 0:TT].rearrange("p (s l) -> p s l", s=4), ALU.mult)
                P.tt("dve", uprev[:, c, :], p2[:, TT:TT + 2], a1[:, TT:TT + 2], ALU.mult)
                P.copy("dve", convs_st[:, :, c, :], u3[:, :, 64:66])
                uv = lambda k: u3[:, :, k:k + 64]
                cvv = cvb[:, 0:TT].rearrange("p (s l) -> p s l", s=4)
            else:
                P.copy("dve", ub[:, 0:2], uprev[:, c, :])
                P.tt("dve", ub[:, 2:2 + TT], p2[:, 0:TT], a1[:, 0:TT], ALU.mult)
                P.copy("dve", uprev[:, c, :], ub[:, TT:TT + 2])
                uv = lambda k: ub[:, k:k + TT]
                cvv = cvb[:, 0:TT]
            cw = lambda k: cst[:, C_CONVW + c * 3 + k:C_CONVW + c * 3 + k + 1]
            P.ts("dve", cvv, uv(0), cw(0), None, ALU.mult)
            P.stt("dve", cvv, uv(1), cw(1), cvv, ALU.mult, ALU.add)
            P.stt("dve", cvv, uv(2), cw(2), cvv, ALU.mult, ALU.add)
            P.act(sgb[:, 0:TT], p4[:, 0:TT], AF.Silu)
            P.tt("dve", bsb[:, 0:TT], p3[:, 0:TT], sgb[:, 0:TT], ALU.mult)
            P.tt("dve", zA[:, c, :], cvb[:, 0:TT], bsb[:, 0:TT], ALU.mult)

    def gla(NT, sample, seqs=None):
        w, _ = wload(win_glr, None, KC, 16)
        pg = bank()
        fm(pg[0:16, 0:NT], w, 0, hT, NT, m=16)
        P.copy("act", glr[0:16, 0:NT], pg[0:16, 0:NT])
        if sample:
            chunks = [(s_ * 128 + j * 64, 64) for s_ in range(NS) for j in range(2)]
            Umat = UB64
        else:
            chunks = [(s_ * 128, 128) for s_ in range(NS)]
            Umat = U128

        def gate1(h):
            g = h % 2
            for s_ in range(NS):
                pxg = bank()
                P.mm(pxg[:, 0:DK], glr[0:17, s_ * 128:(s_ + 1) * 128], wgb[0:17, h * DK:(h + 1) * DK])
                P.act(ex[g][:, s_, :], pxg[:, 0:DK], AF.Exp, scale=-1.0)
                P.act(nla[g][:, s_, :], ex[g][:, s_, :], AF.Ln, bias=cst[:, C_ONE:C_ONE + 1])

        def gate2(h):
            g = h % 2
            for dc in range(2):
                pc = bank()
                for s_ in range(NS):
                    P.mm(pc[:, s_ * 128:(s_ + 1) * 128], nla[g][:, s_, dc * 128:(dc + 1) * 128], Umat)
                P.act(Ep[g][:, dc, :], pc[:, 0:TT], AF.Exp)
                P.act(En[g][:, dc, :], pc[:, 0:TT], AF.Exp, scale=-1.0)
                for ci, (c0, cn) in enumerate(chunks):
                    P.copy("dve", cl[g][:, dc, ci:ci + 1], pc[:, c0 + cn - 1:c0 + cn])
                for ci, (c0, cn) in enumerate(chunks):
                    P.act(Ee[g][:, dc, c0:c0 + cn], pc[:, c0:c0 + cn], AF.Exp, bias=cl[g][:, dc, ci:ci + 1], scale=-1.0)

        pending = []

        def flush():
            while pending:
                zz, h_, cols_ = pending.pop(0)
                pz = bank()
                pzb = pz[:].bitcast(BF16)
                for vc in range(4):
                    P.tr(pzb[:, vc * 128:(vc + 1) * 128], zz[:, vc * 128:(vc + 1) * 128], identb[:])
                P.copy("act", zB[:, h_ * 4:(h_ + 1) * 4, cols_], pzb[:, 0:512].rearrange("p (v l) -> p v l", v=4))

        gate1(0)
        gate2(0)
        for h in range(NH):
            g = h % 2
            if h + 1 < NH:
                gate1(h + 1)
            wqk, _ = wload(win_blk, 16 + 3 * h + 0)
            for dc in range(2):
                pq = bank()
                fm(pq[:, 0:NT], wqk, dc * 128, hT, NT)
                P.stt("dve", qf[:, dc, :], pq[:, 0:TT], DK ** -0.5, Ep[g][:, dc, :], ALU.mult, ALU.mult)
                pk = bank()
                fm(pk[:, 0:NT], wqk, 256 + dc * 128, hT, NT)
                P.tt("dve", kf[:, dc, :], pk[:, 0:TT], En[g][:, dc, :], ALU.mult)
                P.tt("dve", ke[:, dc, :], pk[:, 0:TT], Ee[g][:, dc, :], ALU.mult)
            if h + 1 < NH:
                gate2(h + 1)
            if sample:
                for j in range(2):
                    src = qf[:].rearrange("p d (s j l) -> p d s j l", s=NS, j=2)[:, :, :, j, :]
                    dst = qfz[j][:].rearrange("p d (s j l) -> p d s j l", s=NS, j=2)[:, :, :, j, :]
                    for dc in range(2):
                        P.copy("dve", dst[:, dc], src[:, dc])
            wv, _ = wload(win_blk, 16 + 3 * h + 1)
            for s_ in range(NS):
                pv = bank()
                tm(pv[:, 0:DV], wv, hT, s_)
                P.copy("act", vTM[:, s_, :], pv[:, 0:DV])
            ptr = bank()
            ptrb = ptr[:].bitcast(BF16)
            for s_ in range(NS):
                for dc in range(2):
                    P.tr(ptrb[:, (s_ * 2 + dc) * 128:(s_ * 2 + dc + 1) * 128], ke[:, dc, s_ * 128:(s_ + 1) * 128], identb[:])
            P.copy("act", keTM[:].rearrange("p s d -> p (s d)"), ptrb[:, 0:NS * DK])
            wg, _ = wload(win_blk, 16 + 3 * h + 2)
            for s_ in range(NS):
                pgg = bank()
                tm(pgg[:, 0:DV], wg, hT, s_)
                P.act(gsil[:, s_, :], pgg[:, 0:DV], AF.Silu)
                P.tt("dve", gsil[:, s_, :], gsil[:, s_, :], gnorm_b, ALU.mult)
            ams = []
            for s_ in range(NS):
                cols = slice(s_ * 128, (s_ + 1) * 128)
                pa = bank()
                for dc in range(2):
                    P.mm(pa[:, 0:128], kf[:, dc, cols], qf[:, dc, cols], start=(dc == 0), stop=(dc == 1))
                am = attm[s_]
                P.tt("dve", am[:], pa[:, 0:128], MB64 if sample else M128, ALU.mult)
                ams.append(am)
            for s_ in range(NS):
                cols = slice(s_ * 128, (s_ + 1) * 128)
                if sample:
                    f0, b0, sn = S0f[0], S0b[0], alt(Snew, "snew")
                    for j in range(2):
                        seq = seqs[s_ * 2 + j]
                        src = sg_d[seq, h].rearrange("(dc p) v -> p dc v", p=128)
                        P.dma("sp", f0[:, j], src)
                        P.dma("pool", b0[:, j], src)
                am = ams[s_]
                po = bank()
                if sample:
                    n = 0
                    for j in range(2):
                        for dc in range(2):
                            P.mm(po[:, 0:DV], qfz[j][:, dc, cols], b0[:, j, dc, :], start=(n == 0), stop=False)
                            n += 1
                else:
                    for dc in range(2):
                        P.mm(po[:, 0:DV], qf[:, dc, cols], Sbf[:, h * 2 + dc, :], start=(dc == 0), stop=False)
                pps = []
                if sample:
                    for j in range(2):
                        rows = slice(j * 64, (j + 1) * 64)
                        for dc in range(2):
                            pp = bank()
                            P.mm(pp[:, 0:DV], keTM[rows, s_, dc * 128:(dc + 1) * 128], vTM[rows, s_, :])
                            pps.append((j, dc, pp))
                else:
                    for dc in range(2):
                        pp = bank()
                        P.mm(pp[:, 0:DV], keTM[:, s_, dc * 128:(dc + 1) * 128], vTM[:, s_, :])
                        pps.append((0, dc, pp))
                P.mm(po[:, 0:DV], am[:], vTM[:, s_, :], start=False, stop=True)
                flush()
                if sample:
                    for (j, dc, pp) in pps:
                        ccol = s_ * 128 + j * 64 + 63
                        P.stt("dve", sn[:, j, dc, :], f0[:, j, dc, :], Ep[g][:, dc, ccol:ccol + 1], pp[:, 0:DV],
                              ALU.mult, ALU.add)
                    for j in range(2):
                        seq = seqs[s_ * 2 + j]
                        P.dma("sp", glas_o[seq, h].rearrange("(dc p) v -> p dc v", p=128), sn[:, j], out_dma=True)
                else:
                    for (j, dc, pp) in pps:
                        dec = Ep[g][:, dc, s_ * 128 + 127:s_ * 128 + 128]
                        P.stt("dve", Sst[:, h * 2 + dc, :], Sst[:, h * 2 + dc, :], dec, pp[:, 0:DV], ALU.mult, ALU.add)
                        P.copy("act", Sbf[:, h * 2 + dc, :], Sst[:, h * 2 + dc, :])
                zz = alt(zb, "zb")
                col = (h * NS + s_) % 8
                P.act(zz[:], po[:, 0:DV], AF.Square, scale=float(DV ** -0.5), accum=ssq[:, col:col + 1])
                rsqrt(rs[:, col:col + 1], ssq[:, col:col + 1])
                P.stt("dve", zz[:], po[:, 0:DV], rs[:, col:col + 1], gsil[:, s_, :], ALU.mult, ALU.mult)
                pending.append((zz, h, cols))
        flush()

    def xattn_setup(NT):
        w, _ = wload(win_blk, 28)
        for h in range(XH):
            pq = bank()
            fm(pq[:, 0:NT], w, h * 128, hT, NT)
            P.op("act", lambda e, o=qxT[:, h, :], i=pq[:, 0:TT]: e.mul(o, i, float(128 ** -0.5)),
                 reads=[pq[:, 0:TT]], writes=[qxT[:, h, :]])
        w, _ = wload(win_blk, 29)
        for h in range(XH):
            pg = bank()
            fm(pg[:, 0:NT], w, h * 128, hT, NT)
            P.act(sgx[:, h, :], pg[:, 0:TT], AF.Silu)

    def xattn_units(sample, seqs=None):
        if sample:
            units = [(i * 64, 64, i) for i in range(4)]
        else:
            units = [(s * 128, 128, None) for s in range(NS)]
        for (c0, nt, si) in units:
            if sample:
                KT, V = KTs[0], Vs[0]
                P.dma("pool", KT[:], mkc_d[seqs[si]])
                P.dma("pool", V[:], mvc_d[seqs[si]].rearrange("(mc p) n -> p mc n", p=128))
            else:
                KT, V = KTp, Vp
            pox = LB
            pscs = []
            for h in range(XH):
                psc = bank()
                P.mm(psc[0:nt, 0:NMEM], qxT[:, h, c0:c0 + nt], KT[:, h, :])
                pscs.append(psc)
            pnbs = []
            for h in range(XH):
                psc = pscs[h]
                col = h
                P.op("dve", lambda e, o=mx[0:nt, col:col + 1], i=psc[0:nt, 0:NMEM]: e.reduce_max(o, i, AX.X),
                     reads=[psc[0:nt, 0:NMEM]], writes=[mx[0:nt, col:col + 1]])
                P.ts("dve", mx[0:nt, col:col + 1], mx[0:nt, col:col + 1], -1.0, None, ALU.mult)
                pb = pe_[h]
                P.act(pb[0:nt, :], psc[0:nt, 0:NMEM], AF.Exp, bias=mx[0:nt, col:col + 1], accum=ssum[0:nt, col:col + 1])
                P.op("dve", lambda e, o=ssum[0:nt, col:col + 1]: e.reciprocal(o, o),
                     reads=[ssum[0:nt, col:col + 1]], writes=[ssum[0:nt, col:col + 1]])
                pnb = pn[h]
                P.ts("dve", pnb[0:nt, :], pb[0:nt, :], ssum[0:nt, col:col + 1], None, ALU.mult)
                pnbs.append(pnb)
            yield
            ptts = []
            for h in range(XH):
                pnb = pnbs[h]
                ptb = bank()
                ptbb = ptb[:].bitcast(BF16)
                for mc in range(2):
                    P.tr(ptbb[:, mc * 128:mc * 128 + nt], pnb[0:nt, mc * 128:(mc + 1) * 128], identb[0:nt, 0:nt])
                ptt = pT[h]
                for mc in range(2):
                    P.copy("act", ptt[:, mc, 0:nt], ptbb[:, mc * 128:mc * 128 + nt])
                ptts.append(ptt)
            for h in range(XH):
                ptt = ptts[h]
                for mc in range(2):
                    P.mm(pox[:, h * 128:h * 128 + nt], V[:, mc, h * 128:(h + 1) * 128], ptt[:, mc, 0:nt],
                         start=(mc == 0), stop=(mc == 1))
            for h in range(XH):
                P.tt("dve", zX[:, h, c0:c0 + nt], pox[:, h * 128:h * 128 + nt], sgx[:, h, c0:c0 + nt], ALU.mult)
            yield

    def merge_branch(br, NT):
        zsrc = (zA, zB, zX)[br]
        wo_d = (wouta, woutb, woutx)[br]
        kcn = (KC, KC, 4)[br]
        for g in range(4):
            wm, _ = wload(win_blk, 30 + br * 4 + g)
            s3s = []
            for oc in range(4):
                pm = bank()
                fm(pm[:, 0:NT], wm, oc * 128, hT, NT)
                s3 = sg3[oc]
                P.act(s3[:], pm[:, 0:TT], AF.Sigmoid)
                s3s.append(s3)
            wo, _ = wload(wo_d, g, kcn, 512)
            for oc in range(4):
                o = g * 4 + oc
                py = bank()
                for kc in range(kcn):
                    P.mm(py[:, 0:TT], wo[:, kc, oc * 128:(oc + 1) * 128], zsrc[:, kc, :],
                         start=(kc == 0), stop=(kc == kcn - 1))
                s3 = s3s[oc]
                if br == 0:
                    P.tt("dve", mixf[:, o, :], py[:, 0:TT], s3[:], ALU.mult)
                else:
                    tb = alt(t3, "t3")
                    P.tt("dve", tb[:], py[:, 0:TT], s3[:], ALU.mult)
                    if br == 1:
                        P.tt("dve", mixf[:, o, :], mixf[:, o, :], tb[:], ALU.add)
                    else:
                        P.tt("dve", mixb[:, o, :], mixf[:, o, :], tb[:], ALU.add)
            yield

    def run(gen):
        for _ in gen:
            pass

    def interleave(g1, g2):
        a1 = a2 = True
        while a1 or a2:
            if a1:
                try:
                    next(g1)
                except StopIteration:
                    a1 = False
            if a2:
                try:
                    next(g2)
                except StopIteration:
                    a2 = False

    def final(xi, yT_ap):
        pss = LB
        prev = None
        for g in range(4):
            w, _ = wload(wfin, g)
            for oc in range(4):
                o = g * 4 + oc
                py = bank()
                for kc in range(KC):
                    P.mm(py[:, 0:TT], w[:, kc, oc * 128:(oc + 1) * 128], mixb[:, kc, :],
                         start=(kc == 0), stop=(kc == KC - 1))
                if prev is not None:
                    P.mm(pss[:, 0:TT], onesb[:], prev[0][:, 0:TT], start=(prev[1] == 0), stop=False)
                P.copy("act", yT[:, o, :], py[:, 0:TT])
                s_ = alt(sq, "sq")
                P.act(s_[:, 0:TT], py[:, 0:TT], AF.Square)
                prev = (s_, o)
        P.mm(pss[:, 0:TT], onesb[:], prev[0][:, 0:TT], start=False, stop=True)
        rsqrt(rstd_b[:, 0:TT], pss[:, 0:TT])
        yv = yT_ap.rearrange("(kc p) n -> kc p n", p=128)
        for o in range(KC):
            ot = alt(outT, "outT")
            P.stt("dve", ot[:], yT[:, o, :], cst[:, C_GPOST + o:C_GPOST + o + 1], rstd_b[:, 0:TT], ALU.mult, ALU.mult)
            P.tt("dve", ot[:], ot[:], xinb[xi][:, o, 0:TT], ALU.add)
            P.dma("pool", yv[o], ot[:], out_dma=True)

    if upto >= 0.5:
        stage0(memT, NMEM, C_GMEM)
    if upto >= 1:
        w, _ = wload(wmk)
        for h in range(XH):
            pk = bank()
            fm(pk[:, 0:NMEM], w, h * 128, hT, NMEM)
            P.copy("act", KTp[:, h, :], pk[:, 0:NMEM])
            P.copy("dve", mkst[:, h, :], pk[:, 0:NMEM])
        P.dma("sp", mkT_o.rearrange("(h p) m -> p h m", p=128), mkst[:], out_dma=True)
        w, _ = wload(wmv)
        for mc in range(2):
            pv = bank()
            tm(pv[:, 0:512], w, hT, mc)
            P.copy("act", Vp[:, mc, :], pv[:, 0:512])
            P.copy("dve", mvst[:, mc, :], pv[:, 0:512])
        P.dma("sp", mv_o.rearrange("(mc p) n -> p mc n", p=128), mvst[:], out_dma=True)

    def xloadP(xT_ap):
        P.dma("pool", xinP[:], xT_ap.rearrange("(kc p) n -> p kc n", p=128))

    def stage0P(hTP, part=None):
        if part in (None, 0):
            ps = LB
            for kc in range(KC):
                s_ = alt(sqP, "sqP")
                P.act(s_[:], xinP[:, kc, :], AF.Square)
                P.mm(ps[:, 0:TTP], onesb[:], s_[:], start=(kc == 0), stop=(kc == KC - 1))
            rsqrt(rstdP[:], ps[:, 0:TTP])
        kcs = {None: range(KC), 0: range(0), 1: range(0, KC // 2), 2: range(KC // 2, KC)}[part]
        for kc in kcs:
            P.stt("dve", hTP[:, kc, :], xinP[:, kc, :], cst[:, C_GPRE + kc:C_GPRE + kc + 1],
                  rstdP[:], ALU.mult, ALU.mult)

    def glaP(hTP, mid_hook):
        w, _ = wload(win_glr, None, KC, 16)
        pg = bank()
        fm(pg[0:16, 0:TTP], w, 0, hTP, TTP, m=16)
        P.copy("act", glrP[0:16, :], pg[0:16, 0:TTP])

        def gate1(h):
            g = h % 2
            for s_ in range(NSP):
                pxg = bank()
                P.mm(pxg[:, 0:DK], glrP[0:17, s_ * 128:(s_ + 1) * 128], wgb[0:17, h * DK:(h + 1) * DK])
                P.act(exP[g][:, s_, :], pxg[:, 0:DK], AF.Exp, scale=-1.0)
                P.act(nlaP[g][:, s_, :], exP[g][:, s_, :], AF.Ln, bias=cst[:, C_ONE:C_ONE + 1])

        def gate2(h):
            g = h % 2
            for dc in range(2):
                pc = bank()
                for s_ in range(NSP):
                    P.mm(pc[:, s_ * 128:(s_ + 1) * 128], nlaP[g][:, s_, dc * 128:(dc + 1) * 128], U128)
                P.copy("dve", clP[g][:, dc, :], pc[:, 0:TTP].rearrange("p (s l) -> p s l", l=128)[:, :, 127])
                P.copy("dve", TsP[g][:, dc, NSP - 1:NSP], clP[g][:, dc, NSP - 1:NSP])
                for s_ in range(NSP - 2, -1, -1):
                    P.tt("dve", TsP[g][:, dc, s_:s_ + 1], TsP[g][:, dc, s_ + 1:s_ + 2], clP[g][:, dc, s_:s_ + 1], ALU.add)
                for s_ in range(NSP):
                    P.act(EeP[g][:, dc, s_ * 128:(s_ + 1) * 128], pc[:, s_ * 128:(s_ + 1) * 128], AF.Exp,
                          bias=TsP[g][:, dc, s_:s_ + 1], scale=-1.0)
                P.act(decP[g][:, dc, 0:1], TsP[g][:, dc, 0:1], AF.Exp)

        gate1(0)
        gate2(0)
        for h in range(NH):
            g = h % 2
            if h + 1 < NH:
                gate1(h + 1)
            wk, kc0 = wload(win_blk, 16 + 3 * h + 0, KC, 512, 256, 256)
            for dc in range(2):
                pk = bank()
                fm(pk[:, 0:TTP], wk, kc0 + dc * 128, hTP, TTP)
                P.tt("dve", keP[:, dc, :], pk[:, 0:TTP], EeP[g][:, dc, :], ALU.mult)
            if h + 1 < NH:
                gate2(h + 1)
            wv, _ = wload(win_blk, 16 + 3 * h + 1)
            for s_ in range(NSP):
                pv = bank()
                tm(pv[:, 0:DV], wv, hTP, s_)
                P.copy("act", vTMP[:, s_, :], pv[:, 0:DV])
            ptr = bank()
            ptrb = ptr[:].bitcast(BF16)
            for s_ in range(NSP):
                for dc in range(2):
                    P.tr(ptrb[:, (s_ * 2 + dc) * 128:(s_ * 2 + dc + 1) * 128], keP[:, dc, s_ * 128:(s_ + 1) * 128], identb[:])
            P.copy("act", keTMP[:].rearrange("p s d -> p (s d)"), ptrb[:, 0:NSP * DK])
            if h >= 1:
                mid_hook(h - 1)
            for dc in range(2):
                pp = bank()
                for s_ in range(NSP):
                    P.mm(pp[:, 0:DV], keTMP[:, s_, dc * 128:(dc + 1) * 128], vTMP[:, s_, :],
                         start=(s_ == 0), stop=(s_ == NSP - 1))
                P.stt("dve", Sst[:, h * 2 + dc, :], Sst[:, h * 2 + dc, :], decP[g][:, dc, 0:1], pp[:, 0:DV],
                      ALU.mult, ALU.add)

    if phase1:
        P.memset("dve", glrP[:], 1.0)
        NPT = 3 * SEG // TTP
        pcl = [(win_blk, c) for c in range(KC)]
        for g_ in range(4):
            pcl += [(win_blk, 30 + g_), (wouta, g_)]
        pcl += [(win_blk, 16 + 3 * h_ + 2) for h_ in range(NH)]
        pcl += [(win_blk, 28), (win_blk, 29)]
        for g_ in range(4):
            pcl += [(win_blk, 34 + g_), (woutb, g_)]
        for g_ in range(4):
            pcl += [(win_blk, 38 + g_), (woutx, g_)]
        pcl += [(wfin, g_) for g_ in range(4)]
        xloadP(xpre[:, 0:TTP])
        stage0P(hTPb[0])
        for t in range(NPT):
            def hook(part, t=t):
                if t + 1 < NPT:
                    stage0P(hTPb[(t + 1) % 2], part)
            if t + 1 < NPT:
                xloadP(xpre[:, (t + 1) * TTP:(t + 2) * TTP])
            for _ in range(3):
                if pcl:
                    precast(*pcl.pop(0))
            glaP(hTPb[t % 2], hook)
        while pcl:
            precast(*pcl.pop(0))
        P.copy("act", Sbf[:].rearrange("p a v -> p (a v)"), Sst[:].rearrange("p a v -> p (a v)"))
    P.memset("dve", qfz[0][:], 0.0)
    P.memset("dve", qfz[1][:], 0.0)

    seqs = [0, 1, 2, 3]
    if upto >= 2:
        stage0(xsT, TT + NHALO, C_GPRE)
    if upto >= 3:
        branch_a(TT + NHALO, True)
    if upto >= 4:
        run(merge_branch(0, TT + NHALO))
    if upto >= 5:
        gla(TT + NHALO, True, seqs=seqs)
    if upto >= 8:
        xattn_setup(TT + NHALO)
        interleave(xattn_units(True, seqs), merge_branch(1, TT + NHALO))
        run(merge_branch(2, TT + NHALO))
    if upto >= 9:
        final(0, ysT)
        P.dma("sp", convs_o[:, :, :, :], convs_st[:], out_dma=True)
    if upto < 10:
        NTP = 0

    if NTP > 0:
        xload(xpT[:, 0:TT], TT, 1)
    for t in range(NTP):
        xi = (t + 1) % 2
        xs_ = xpT[:, t * TT:(t + 1) * TT]
        stage0(xs_, TT, C_GPRE, load=False, xi=xi)
        branch_a(TT, False)
        run(merge_branch(0, TT))
        gla(TT, False)
        xattn_setup(TT)
        interleave(xattn_units(False), merge_branch(1, TT))
        run(merge_branch(2, TT))
        if t + 1 < NTP:
            xload(xpT[:, (t + 1) * TT:(t + 2) * TT], TT, 1 - xi)
        final(xi, ypT[:, t * TT:(t + 1) * TT])
    P.dma("sp", convp_o[:, :, :], uprev[:], out_dma=True)
    P.dma("sp", glap_o.rearrange("h (dc p) v -> p (h dc) v", p=128), Sst[:], out_dma=True)

    P.emit(st)
    st.close()
    return nc


def _tile_w(w, ncol=512):
    K, N = w.shape
    return np.ascontiguousarray(w.reshape(K // 128, 128, N // ncol, ncol).transpose(2, 1, 0, 3))


def _fm_vec(v):
    return np.ascontiguousarray(v.reshape(KC, 128).T)


def _const_pack(g_pre, g_post, g_mem, conv_w, gla_norm, core):
    c = np.zeros((128, CST_N), np.float32)
    i = np.arange(128)
    c[:, C_IDENT:C_IDENT + 128] = np.eye(128, dtype=np.float32)
    up = (i[:, None] <= i[None, :]).astype(np.float32)
    blk = ((i[:, None] // 64) == (i[None, :] // 64)).astype(np.float32)
    c[:, C_U128:C_U128 + 128] = up * (-1.0 / GATE_TAU)
    c[:, C_UB64:C_UB64 + 128] = up * blk * (-1.0 / GATE_TAU)
    c[:, C_M128:C_M128 + 128] = up
    c[:, C_MB64:C_MB64 + 128] = up * blk
    c[:, C_ONES:C_ONES + 128] = 1.0 / D
    c[:, C_GPRE:C_GPRE + 16] = _fm_vec(g_pre)
    c[:, C_GPOST:C_GPOST + 16] = _fm_vec(g_post)
    c[:, C_GMEM:C_GMEM + 16] = _fm_vec(g_mem)
    c[:, C_CONVW:C_CONVW + 48] = conv_w.reshape(3, KC, 128).transpose(2, 1, 0).reshape(128, 48)
    c[:, C_GNORM:C_GNORM + 512] = gla_norm[None, :]
    c[:, C_EPS] = EPS
    c[:, C_ONE] = 1.0
    seg = core % 4
    base = core - seg
    for ii in range(NCORE):
        if base <= ii < core:
            c[:, C_MASK + ii] = 1.0
            for m in range(ii + 1, core):
                c[:, C_CMAT + ii * 8 + m] = 1.0
    return c


_CACHE = {}


def kernel(x_prompt, x_sample, cache_conv, state_gla, cache_mem_k, cache_mem_v, mem_prompt,
           g_pre, g_post, g_mem, w_in, conv_w, w_gate_up, b_gate, gla_norm,
           w_mem_k, w_mem_v, w_out_a, w_out_b, w_out_x, w_final, _phase1=True, _ntiles=None, _upto=99):
    f = lambda a: np.asarray(a, dtype=np.float32)
    x_prompt, x_sample, cache_conv, state_gla = f(x_prompt), f(x_sample), f(cache_conv), f(state_gla)
    cache_mem_k, cache_mem_v, mem_prompt = f(cache_mem_k), f(cache_mem_v), f(mem_prompt)
    B, SEQ, _ = x_prompt.shape
    SEG = SEQ // 4
    key = (SEG, _phase1, _ntiles, _upto)
    if key not in _CACHE:
        _CACHE[key] = build_program(SEG, phase1=_phase1, n_prompt_tiles=_ntiles, upto=_upto)
    nc = _CACHE[key]

    w_in0 = f(w_in)[0]
    o_ain, o_ab, o_ac, o_ag = 0, 2048, 4096, 6144
    o_q, o_k, o_v, o_lr, o_gg, o_xq, o_xg, o_m = 8192, 9216, 10240, 12288, 12304, 14352, 14864, 15376
    cols = []
    for c in range(KC):
        for o in (o_ain, o_ac, o_ab, o_ag):
            cols.append(np.arange(o + c * 128, o + (c + 1) * 128))
    for h in range(NH):
        cols.append(np.arange(o_q + h * DK, o_q + (h + 1) * DK))
        cols.append(np.arange(o_k + h * DK, o_k + (h + 1) * DK))
        cols.append(np.arange(o_v + h * DV, o_v + (h + 1) * DV))
        cols.append(np.arange(o_gg + h * DV, o_gg + (h + 1) * DV))
    cols.append(np.arange(o_xq, o_xq + 512))
    cols.append(np.arange(o_xg, o_xg + 512))
    cols.append(np.arange(o_m, o_m + 3 * D))
    perm = np.concatenate(cols)
    assert perm.size == 42 * 512
    win_blk = _tile_w(w_in0[:, perm])
    win_glr = np.ascontiguousarray(w_in0[:, o_lr:o_lr + 16].reshape(KC, 128, 16).transpose(1, 0, 2))
    wouta = _tile_w(f(w_out_a)[0])
    woutb = _tile_w(f(w_out_b)[0])
    woutx = _tile_w(f(w_out_x)[0])
    wfin = _tile_w(f(w_final)[0])
    wmk = _tile_w(f(w_mem_k)[0])[0]
    wmv = _tile_w(f(w_mem_v)[0])[0]
    wgb = np.zeros((32, 1024), np.float32)
    wgb[0:16] = f(w_gate_up)[0]
    wgb[16] = f(b_gate)[0]

    in_maps = []
    for core in range(NCORE):
        b, seg = core // 4, core % 4
        xp = x_prompt[b, seg * SEG:(seg + 1) * SEG]
        halo = x_prompt[b, seg * SEG - 2:seg * SEG] if seg > 0 else np.zeros((2, D), np.float32)
        xs = np.concatenate([x_sample[4 * core:4 * core + 4].reshape(TT, D), halo], axis=0)
        pre = np.zeros((D, 3 * SEG), np.float32)
        if seg > 0:
            pre[:, (3 - seg) * SEG:] = x_prompt[b, 0:seg * SEG].T
        in_maps.append({
            "xpre": pre,
            "xpT": np.ascontiguousarray(xp.T),
            "xsT": np.ascontiguousarray(xs.T),
            "memT": np.ascontiguousarray(mem_prompt[b].T),
            "cst": _const_pack(f(g_pre)[0], f(g_post)[0], f(g_mem)[0], f(conv_w)[0], f(gla_norm)[0], core),
            "wgb": wgb,
            "win_blk": win_blk, "win_glr": win_glr, "wouta": wouta, "woutb": woutb, "woutx": woutx,
            "wfin": wfin, "wmk": wmk, "wmv": wmv,
            "convc": np.ascontiguousarray(
                cache_conv[0, 4 * core:4 * core + 4].reshape(4, 2, KC, 128).transpose(3, 0, 2, 1)),
            "sg": np.ascontiguousarray(state_gla[0, 4 * core:4 * core + 4]),
            "mkc": np.ascontiguousarray(cache_mem_k[0, 4 * core:4 * core + 4].transpose(0, 3, 2, 1)),
            "mvc": np.ascontiguousarray(cache_mem_v[0, 4 * core:4 * core + 4].reshape(4, NMEM, 512)),
        })
    if not _phase1:
        for m in in_maps:
            m.pop("xpre")
    res = run_bass_kernel_spmd(nc, in_maps, core_ids=list(range(NCORE)))
    R = res.results

    y_prompt = np.empty((B, SEQ, D), np.float32)
    y_sample = np.empty((32, 64, D), np.float32)
    conv_s = np.empty((1, 32, 2, D), np.float32)
    gla_s = np.empty((1, 32, NH, DK, DV), np.float32)
    for core in range(NCORE):
        b, seg = core // 4, core % 4
        y_prompt[b, seg * SEG:(seg + 1) * SEG] = R[core]["ypT"].T
        y_sample[4 * core:4 * core + 4] = R[core]["ysT"].T.reshape(4, 64, D)
        conv_s[0, 4 * core:4 * core + 4] = R[core]["convs"].transpose(1, 3, 2, 0).reshape(4, 2, D)
        gla_s[0, 4 * core:4 * core + 4] = R[core]["glas"]
    conv_p = np.stack([R[c]["convp"].transpose(2, 1, 0).reshape(2, D) for c in (3, 7)])[None]
    gla_p = np.stack([R[c]["glap"] for c in (3, 7)])[None]
    mem_k = np.stack([R[c]["mkT"].T.reshape(NMEM, XH, 128) for c in (0, 4)])[None]
    mem_v = np.stack([R[c]["mvo"].reshape(NMEM, XH, 128) for c in (0, 4)])[None]
    return (y_prompt, y_sample, conv_p.astype(np.float32), gla_p.astype(np.float32),
            mem_k.astype(np.float32), mem_v.astype(np.float32), conv_s, gla_s)
```

```python
import numpy as np
from contextlib import ExitStack

import concourse.bass as bass
import concourse.mybir as mybir
from concourse.bass_utils import run_bass_kernel_spmd

F32 = mybir.dt.float32
BF16 = mybir.dt.bfloat16
AF = mybir.ActivationFunctionType
ALU = mybir.AluOpType
AX = mybir.AxisListType
DSZ = {F32: 4, BF16: 2}

D = 2048
KC = 16
NCORE = 8
TT = 256
NS = TT // 128
NHALO = 2
EPS = 1e-6
NH = 4
DK = 256
DV = 512
NMEM = 256
XH = 4
GATE_TAU = 16.0

C_IDENT = 0
C_U128 = 128
C_UB64 = 256
C_M128 = 384
C_MB64 = 512
C_ONES = 640
C_GPRE = 768
C_GPOST = 784
C_GMEM = 800
C_CONVW = 816
C_GNORM = 864
C_CMAT = 1376
C_MASK = 1440
C_EPS = 1448
C_ONE = 1449
CST_N = 1456


def _iv(ap):
    t = ap.tensor
    name = t.name
    a = ap.ap
    off = int(ap.offset)
    dsz = DSZ.get(ap.dtype, 4)
    tn = type(t).__name__
    if "DRam" in tn:
        if not name.startswith("wb_"):
            return None
        span = sum(int(s) * (int(c) - 1) for s, c in a) + 1
        return name, 0, 1, off * dsz, (off + span) * dsz
    pstride, npart = int(a[0][0]), int(a[0][1])
    p0 = off // pstride
    f0 = off % pstride
    span = sum(int(s) * (int(c) - 1) for s, c in a[1:]) + 1
    return name, p0, p0 + npart, f0 * dsz, (f0 + span) * dsz


def _is_psum(ap):
    return "PSum" in type(ap.tensor).__name__


class Prog:
    COMPUTE = ("pe", "act", "dve", "pool")
    QUEUES = ("sp", "pool", "act")
    NDSEM = 8

    def __init__(self, nc):
        self.nc = nc
        self.ops = []
        self.recs = {}
        self.psrec = {}
        self.dma_count = {q: 0 for q in self.QUEUES}
        self.dma_ops = {q: [] for q in self.QUEUES}
        self.out_dmas = []

    def _deps(self, eng, reads, writes, is_dma):
        deps = set()
        for ap in reads:
            iv = _iv(ap)
            if iv is None:
                continue
            name, p0, p1, b0, b1 = iv
            rec = self.recs.setdefault(name, {"w": [], "r": []})
            for (q0, q1, c0, c1, oid) in rec["w"]:
                if q0 < p1 and p0 < q1 and c0 < b1 and b0 < c1:
                    deps.add(oid)
        for ap in writes:
            iv = _iv(ap)
            if iv is None:
                continue
            name, p0, p1, b0, b1 = iv
            rec = self.recs.setdefault(name, {"w": [], "r": []})
            for (q0, q1, c0, c1, oid) in rec["w"]:
                if q0 < p1 and p0 < q1 and c0 < b1 and b0 < c1:
                    deps.add(oid)
            for (q0, q1, c0, c1, oid, _e) in rec["r"]:
                if q0 < p1 and p0 < q1 and c0 < b1 and b0 < c1:
                    deps.add(oid)
        return deps

    def _record(self, oid, eng, reads, writes, is_dma):
        for ap in writes:
            iv = _iv(ap)
            if iv is None:
                continue
            name, p0, p1, b0, b1 = iv
            rec = self.recs[name]
            rec["w"] = [r for r in rec["w"]
                        if not (p0 <= r[0] and r[1] <= p1 and b0 <= r[2] and r[3] <= b1)]
            rec["r"] = [r for r in rec["r"]
                        if not (p0 <= r[0] and r[1] <= p1 and b0 <= r[2] and r[3] <= b1)]
            rec["w"].append((p0, p1, b0, b1, oid))
        for ap in reads:
            iv = _iv(ap)
            if iv is None:
                continue
            name, p0, p1, b0, b1 = iv
            rec = self.recs[name]
            if not is_dma:
                rec["r"] = [r for r in rec["r"]
                            if not (r[5] == eng and r[0] == p0 and r[1] == p1 and r[2] == b0 and r[3] == b1)]
            rec["r"].append((p0, p1, b0, b1, oid, eng if not is_dma else "dma"))

    def op(self, eng, fn, reads=(), writes=(), dma=False, out_dma=False, extra_deps=()):
        reads = [r for r in reads if r is not None and not isinstance(r, (int, float))]
        writes = [w for w in writes if w is not None]
        oid = len(self.ops)
        ps_reads = [r for r in reads if _is_psum(r)]
        ps_writes = [w for w in writes if _is_psum(w)]
        reads = [r for r in reads if not _is_psum(r)]
        writes = [w for w in writes if not _is_psum(w)]
        deps = self._deps(eng, reads, writes, dma)
        deps.update(extra_deps)
        for aps, isw in ((ps_reads, False), (ps_writes, True)):
            for ap in aps:
                name, p0, p1, b0, b1 = _iv(ap)
                lst = self.psrec.setdefault(name, [])
                keep = []
                for (q0, q1, c0, c1, o2, e2, w2) in lst:
                    if e2 != eng:
                        deps.add(o2)
                    elif (isw or w2) and q0 < p1 and p0 < q1 and c0 < b1 and b0 < c1:
                        deps.add(o2)
                    elif not (q0 == p0 and q1 == p1 and c0 == b0 and c1 == b1 and w2 == isw):
                        keep.append((q0, q1, c0, c1, o2, e2, w2))
                keep.append((p0, p1, b0, b1, oid, eng, isw))
                self.psrec[name] = keep
        if not dma and eng == "pe":
            deps = {d for d in deps if not (self.ops[d]["eng"] == "pe" and not self.ops[d]["dma"])}
        latest = {}
        for d_ in list(deps):
            od = self.ops[d_]
            if not od["dma"]:
                e2 = od["eng"]
                if e2 in latest:
                    if d_ > latest[e2]:
                        deps.discard(latest[e2])
                        latest[e2] = d_
                    else:
                        deps.discard(d_)
                else:
                    latest[e2] = d_
        o = {"eng": eng, "fn": fn, "deps": deps, "dma": dma}
        if dma:
            k = self.dma_count[eng]
            self.dma_count[eng] += 1
            o["dsem"] = k % self.NDSEM
            o["dval"] = 16 * (k // self.NDSEM + 1)
            if k >= self.NDSEM:
                deps.add(self.dma_ops[eng][k - self.NDSEM])
            self.dma_ops[eng].append(oid)
            if out_dma:
                self.out_dmas.append(oid)
        self.ops.append(o)
        self._record(oid, eng, reads, writes, dma)
        return oid

    def mm(self, out, lhsT, rhs, start=True, stop=True):
        return self.op("pe", lambda e: e.matmul(out, lhsT, rhs, start=start, stop=stop),
                       reads=[lhsT, rhs], writes=[out])

    def tr(self, out, in_, ident):
        return self.op("pe", lambda e: e.transpose(out, in_, ident), reads=[in_, ident], writes=[out])

    def act(self, out, in_, func, bias=0.0, scale=1.0, accum=None):
        kw = {}
        if accum is not None:
            kw["accum_out"] = accum
        return self.op("act", lambda e: e.activation(out=out, in_=in_, func=func, bias=bias, scale=scale, **kw),
                       reads=[in_, bias, scale], writes=[out, accum])

    def tt(self, eng, out, a, b, op):
        return self.op(eng, lambda e: e.tensor_tensor(out, a, b, op), reads=[a, b], writes=[out])

    def ts(self, eng, out, a, s1, s2, op0, op1=None):
        if op1 is None:
            return self.op(eng, lambda e: e.tensor_scalar(out, a, s1, None, op0), reads=[a, s1], writes=[out])
        return self.op(eng, lambda e: e.tensor_scalar(out, a, s1, s2, op0, op1), reads=[a, s1, s2], writes=[out])

    def stt(self, eng, out, in0, scalar, in1, op0, op1):
        return self.op(eng, lambda e: e.scalar_tensor_tensor(out, in0, scalar, in1, op0, op1),
                       reads=[in0, scalar, in1], writes=[out])

    def copy(self, eng, out, in_):
        if eng == "act":
            return self.op("act", lambda e: e.copy(out, in_), reads=[in_], writes=[out])
        return self.op(eng, lambda e: e.tensor_copy(out, in_), reads=[in_], writes=[out])

    def memset(self, eng, ap, val):
        return self.op(eng, lambda e: e.memset(ap, val), writes=[ap])

    def dma(self, q, out, in_, out_dma=False):
        return self.op(q, lambda e: e.dma_start(out=out, in_=in_), reads=[in_], writes=[out], dma=True,
                       out_dma=out_dma)

    def emit(self, stack):
        nc = self.nc
        ops = self.ops
        engs = ("pe", "act", "dve", "pool", "sp")
        ops.append({"eng": "sp", "fn": None, "deps": set(self.out_dmas), "dma": False})
        targets = set()
        for o in ops:
            targets.update(o["deps"])
        csem = {e: stack.enter_context(nc.semaphore("c_" + e)) for e in engs}
        dsem = {q: [stack.enter_context(nc.semaphore("d_%s%d" % (q, i))) for i in range(self.NDSEM)]
                for q in self.QUEUES}
        cnt = {e: 0 for e in engs}
        for i, o in enumerate(ops):
            if o["dma"]:
                o["ev"] = (("d", o["eng"], o["dsem"]), o["dval"])
            elif i in targets:
                cnt[o["eng"]] += 1
                o["ev"] = (("c", o["eng"]), cnt[o["eng"]])
            else:
                o["ev"] = None
        per_eng = {e: [] for e in engs}
        for i, o in enumerate(ops):
            per_eng[o["eng"]].append(i)
        handles = {"pe": "tensor", "act": "scalar", "dve": "vector", "pool": "gpsimd", "sp": "sync"}

        def semof(key):
            return csem[key[1]] if key[0] == "c" else dsem[key[1]][key[2]]

        def body(ename):
            def f(e):
                waited = {}
                for i in per_eng[ename]:
                    o = ops[i]
                    need = {}
                    for d in o["deps"]:
                        k, v = ops[d]["ev"]
                        if waited.get(k, 0) >= v:
                            continue
                        if need.get(k, 0) < v:
                            need[k] = v
                    for k, v in need.items():
                        e.wait_ge(semof(k), v)
                        waited[k] = v
                    if o["fn"] is None:
                        continue
                    ins = o["fn"](e)
                    if o["dma"]:
                        ins.then_inc(dsem[o["eng"]][o["dsem"]], 16)
                    elif o["ev"] is not None:
                        ins.then_inc(csem[ename], 1)
            return f

        block = stack.enter_context(nc.Block())
        for ename in engs:
            getattr(block, handles[ename])(body(ename))


def build_program(SEG, phase1=True, n_prompt_tiles=None, upto=99):
    NTP = SEG // TT if n_prompt_tiles is None else n_prompt_tiles
    nc = bass.Bass("TRN2", target_bir_lowering=False)
    P = Prog(nc)
    st = ExitStack()

    def din(name, shape, dt=F32):
        return nc.dram_tensor(name, list(shape), dt, kind="ExternalInput").ap()

    def dout(name, shape, dt=F32):
        return nc.dram_tensor(name, list(shape), dt, kind="ExternalOutput").ap()

    def sb(name, shape, dt=F32):
        return st.enter_context(nc.sbuf_tensor(name, list(shape), dt))

    xpT = din("xpT", [D, SEG])
    xsT = din("xsT", [D, TT + NHALO])
    xpre = din("xpre", [D, 3 * SEG]) if phase1 else None
    memT = din("memT", [D, NMEM])
    cst_d = din("cst", [128, CST_N])
    wgb_d = din("wgb", [32, 1024])
    win_blk = din("win_blk", [42, 128, KC, 512])
    win_glr = din("win_glr", [128, KC, 16])
    wouta = din("wouta", [4, 128, KC, 512])
    woutb = din("woutb", [4, 128, KC, 512])
    woutx = din("woutx", [4, 128, 4, 512])
    wfin = din("wfin", [4, 128, KC, 512])
    wmk = din("wmk", [128, KC, 512])
    wmv = din("wmv", [128, KC, 512])
    convc_d = din("convc", [128, 4, KC, 2])
    sg_d = din("sg", [4, NH, DK, DV])
    mkc_d = din("mkc", [4, 128, XH, NMEM])
    mvc_d = din("mvc", [4, NMEM, 512])

    ypT = dout("ypT", [D, SEG])
    ysT = dout("ysT", [D, TT])
    convp_o = dout("convp", [128, KC, 2])
    convs_o = dout("convs", [128, 4, KC, 2])
    glap_o = dout("glap", [NH, DK, DV])
    glas_o = dout("glas", [4, NH, DK, DV])
    mkT_o = dout("mkT", [512, NMEM])
    mv_o = dout("mvo", [NMEM, 512])

    NTM = TT + NHALO
    TTP = 512
    NSP = TTP // 128
    cst = sb("cst_s", [128, CST_N])
    wgb = sb("wgb_s", [32, 1024], BF16)
    identb = sb("identb", [128, 128], BF16)
    onesb = sb("onesb", [128, 128], BF16)
    sq = [sb("sq%d" % i, [128, NTM], BF16) for i in range(2)]
    rstd_b = sb("rstd_b", [128, NTM])
    hT = sb("hT", [128, KC, NTM], BF16)
    NW = 3
    wsl = [sb("wsl%d" % i, [128, KC, 512], BF16) for i in range(NW)]
    uprev = sb("uprev", [128, KC, 2])
    convc = sb("convc_s", [128, 4, KC, 2])
    convs_st = sb("convs_st", [128, 4, KC, 2])
    glr = sb("glr", [32, NTM], BF16)
    ssq = sb("ssq", [128, 8])
    rs = sb("rs", [128, 8])
    Sst = sb("Sst", [128, NH * 2, DV])
    Sbf = sb("Sbf", [128, NH * 2, DV], BF16)
    KTp = sb("KTp", [128, XH, NMEM], BF16)
    Vp = sb("Vp", [128, 2, 512], BF16)
    mx = sb("mx", [128, 8])
    ssum = sb("ssum", [128, 8])

    ARENA_B = 106 * 1024
    arena = sb("arena", [128, ARENA_B // 4])

    def av(ptr, shape, dt=F32):
        n = 1
        for d_ in shape[1:]:
            n *= d_
        nb = (n * DSZ[dt] + 31) // 32 * 32
        a = arena[:, ptr[0] // 4:(ptr[0] + nb) // 4]
        ptr[0] += nb
        assert ptr[0] <= ARENA_B, ptr[0]
        if dt == BF16:
            a = a.bitcast(BF16)
        a = a[:, 0:n]
        if len(shape) == 3:
            a = a.rearrange("p (a b) -> p a b", a=shape[1])
        elif len(shape) == 4:
            a = a.rearrange("p (a b c) -> p a b c", a=shape[1], b=shape[2])
        return a

    mp = [0]
    big = av(mp, [128, KC, TT])
    xin0 = av(mp, [128, KC, NTM])
    zA = av(mp, [128, KC, TT], BF16)
    zB = zA
    zX = av(mp, [128, 4, TT], BF16)
    mixf = big
    mixb = zA
    yT = big
    qfz = [av(mp, [128, 2, TT], BF16) for i in range(2)]
    sbase = mp[0]
    S0f = [av(mp, [128, 2, 2, DV]) for i in range(1)]
    S0b = [av(mp, [128, 2, 2, DV], BF16) for i in range(1)]
    sbig = av(mp, [128, NH * 2, DV])
    Snew = [sbig[:, 4 * i:4 * i + 4, :].rearrange("p (j d) v -> p j d v", j=2) for i in range(2)]
    mkst = sbig[:, 0:2, :].rearrange("p a (h m) -> p (a h) m", h=2)
    mvst = sbig[:, 2:4, :]
    xin1 = av([sbase], [128, KC, TT])
    xinb = [xin0, xin1]
    gb = mp[0]
    ex = [av(mp, [128, NS, DK]) for i in range(2)]
    nla = [av(mp, [128, NS, DK]) for i in range(2)]
    Ep = [av(mp, [128, 2, TT]) for i in range(2)]
    En = [av(mp, [128, 2, TT]) for i in range(2)]
    Ee = [av(mp, [128, 2, TT]) for i in range(2)]
    cl = [av(mp, [128, 2, 4]) for i in range(2)]
    qf = av(mp, [128, 2, TT], BF16)
    kf = av(mp, [128, 2, TT], BF16)
    ke = av(mp, [128, 2, TT], BF16)
    keTM = av(mp, [128, NS, DK], BF16)
    vTM = av(mp, [128, NS, DV], BF16)
    gsil = av(mp, [128, NS, DV])
    attm = [av(mp, [128, 128], BF16) for i in range(2)]
    zb = [av(mp, [128, DV], BF16) for i in range(2)]
    top = mp[0]
    mp = [gb]
    t1 = [av(mp, [128, NTM]) for i in range(2)]
    ubuf = [av(mp, [128, 4 * 66]) for i in range(2)]
    cv = [av(mp, [128, TT]) for i in range(2)]
    sgA = [av(mp, [128, NTM]) for i in range(2)]
    bsA = [av(mp, [128, TT]) for i in range(2)]
    top = max(top, mp[0])
    mp = [gb + 6 * 1024 + 4 * 1024]
    qxT = av(mp, [128, XH, TT], BF16)
    sgx = av(mp, [128, XH, TT], BF16)
    KTs = [av(mp, [128, XH, NMEM], BF16) for i in range(1)]
    Vs = [av(mp, [128, 2, 512], BF16) for i in range(1)]
    pe_ = [av(mp, [128, NMEM]) for i in range(4)]
    pn = [av(mp, [128, NMEM], BF16) for i in range(4)]
    pT = [av(mp, [128, 2, 128], BF16) for i in range(4)]
    top = max(top, mp[0])
    mp = [gb]
    sg3 = [av(mp, [128, TT]) for i in range(4)]
    t3 = [av(mp, [128, TT]) for i in range(2)]
    outT = [av(mp, [128, TT]) for i in range(4)]
    top = max(top, mp[0])
    mp = [top]

    pp_ = [0]
    xinP = av(pp_, [128, KC, TTP])
    hTPb = [av(pp_, [128, KC, TTP], BF16) for i in range(2)]
    sqP = [av(pp_, [128, TTP], BF16) for i in range(2)]
    rstdP = av(pp_, [128, TTP])
    glrP = av(pp_, [32, TTP], BF16)
    exP = [av(pp_, [128, NSP, DK]) for i in range(2)]
    nlaP = [av(pp_, [128, NSP, DK]) for i in range(2)]
    EeP = [av(pp_, [128, 2, TTP]) for i in range(2)]
    clP = [av(pp_, [128, 2, NSP]) for i in range(2)]
    decP = [av(pp_, [128, 2, NSP]) for i in range(2)]
    TsP = [av(pp_, [128, 2, NSP]) for i in range(2)]
    keP = av(pp_, [128, 2, TTP], BF16)
    keTMP = av(pp_, [128, NSP, DK], BF16)
    vTMP = av(pp_, [128, NSP, DV], BF16)
    print("arena main %d prefix %d of %d; sbuf left %d" % (mp[0], pp_[0], ARENA_B, nc.sbuf_bytes_remaining))

    banks = [st.enter_context(nc.psum_tensor("bank%d" % i, [128, 512], F32)) for i in range(8)]
    bctr = [0]

    ROT = (0, 1, 2, 3, 4, 5, 7)

    def bank():
        b = banks[ROT[bctr[0] % len(ROT)]]
        bctr[0] += 1
        return b

    LB = banks[6]

    wctr = [0]

    scratch = {}
    cached = set()

    def wload(src_full, idx=None, kc=KC, ncol=512, c0=0, nc_=None):
        name = src_full.tensor.name
        if name not in scratch:
            scratch[name] = nc.dram_tensor("wb_" + name, list(src_full.shape), BF16, kind="Internal").ap()
        sc = scratch[name]
        src = src_full if idx is None else src_full[idx]
        scv = sc if idx is None else sc[idx]
        s = wsl[wctr[0] % NW]
        wctr[0] += 1
        if (name, idx) not in cached:
            cached.add((name, idx))
            P.dma("pool", s[:, 0:kc, 0:ncol], src)
            P.dma("pool", scv, s[:, 0:kc, 0:ncol])
            return s, c0
        if nc_ is None:
            P.dma("sp", s[:, 0:kc, 0:ncol], scv)
            return s, c0
        P.dma("sp", s[:, 0:kc, 0:nc_], scv[:, :, c0:c0 + nc_])
        return s, 0

    def precast(src_full, idx):
        name = src_full.tensor.name
        if name not in scratch:
            scratch[name] = nc.dram_tensor("wb_" + name, list(src_full.shape), BF16, kind="Internal").ap()
        if (name, idx) in cached:
            return
        cached.add((name, idx))
        P.dma("pool", scratch[name][idx], src_full[idx])

    rr = {"a": 0}

    def alt(lst, key):
        rr[key] = rr.get(key, 0) + 1
        return lst[rr[key] % len(lst)]

    c_ = lambda o, n: cst[:, o:o + n]
    ident_f = c_(C_IDENT, 128)
    U128 = c_(C_U128, 128)
    UB64 = c_(C_UB64, 128)
    M128 = c_(C_M128, 128)
    MB64 = c_(C_MB64, 128)
    onesD = c_(C_ONES, 128)
    gnorm_b = c_(C_GNORM, 512)

    P.dma("sp", cst[:], cst_d[:, :])
    P.dma("pool", wgb[:], wgb_d[:, :])
    P.dma("sp", convc[:], convc_d[:, :, :, :])
    P.copy("dve", identb[:], ident_f)
    P.copy("dve", onesb[:], onesD)
    P.memset("dve", glr[:], 1.0)
    P.memset("dve", Sst[:], 0.0)
    P.memset("dve", Sbf[:], 0.0)
    P.memset("dve", uprev[:], 0.0)

    def xload(xT_ap, NT, xi=0):
        P.dma("pool", xinb[xi][:, :, 0:NT], xT_ap.rearrange("(kc p) n -> p kc n", p=128))

    def stage0(xT_ap, NT, gcol, load=True, xi=0):
        xin = xinb[xi]
        if load:
            xload(xT_ap, NT, xi)
        ps = LB
        for kc in range(KC):
            s = alt(sq, "sq")
            P.act(s[:, 0:NT], xin[:, kc, 0:NT], AF.Square)
            P.mm(ps[:, 0:NT], onesb[:], s[:, 0:NT], start=(kc == 0), stop=(kc == KC - 1))
        rsqrt(rstd_b[:, 0:NT], ps[:, 0:NT])
        for kc in range(KC):
            P.stt("dve", hT[:, kc, 0:NT], xin[:, kc, 0:NT], cst[:, gcol + kc:gcol + kc + 1],
                  rstd_b[:, 0:NT], ALU.mult, ALU.mult)

    def rsqrt(out, in_):
        P.act(out, in_, AF.Ln, bias=cst[0:out.shape[0], C_EPS:C_EPS + 1])
        P.act(out, out, AF.Exp, scale=-0.5)

    def fm(ps_ap, w, col0, rhs, NT, kcn=KC, m=128):
        for kc in range(kcn):
            P.mm(ps_ap, w[:, kc, col0:col0 + m], rhs[:, kc, 0:NT], start=(kc == 0), stop=(kc == kcn - 1))

    def tm(ps_ap, w, lhs, sub, ncol=512, kcn=KC):
        for kc in range(kcn):
            P.mm(ps_ap, lhs[:, kc, sub * 128:(sub + 1) * 128], w[:, kc, 0:ncol],
                 start=(kc == 0), stop=(kc == kcn - 1))

    def branch_a(NT, sample):
        for c in range(KC):
            w, _ = wload(win_blk, c)
            p1, p2, p3, p4 = bank(), bank(), bank(), bank()
            fm(p1[:, 0:NT], w, 0, hT, NT)
            fm(p2[:, 0:NT], w, 128, hT, NT)
            fm(p3[:, 0:NT], w, 256, hT, NT)
            fm(p4[:, 0:NT], w, 384, hT, NT)
            a1 = alt(t1, "t1")
            ub = alt(ubuf, "ub")
            cvb = alt(cv, "cv")
            sgb = alt(sgA, "sgA")
            bsb = alt(bsA, "bsA")
            P.copy("act", a1[:, 0:NT], p1[:, 0:NT])
            if sample:
                u3 = ub[:, 0:264].rearrange("p (s l) -> p s l", s=4)
                P.copy("dve", u3[:, :, 0:2], convc[:, :, c, :])
                P.tt("dve", u3[:, :, 2:66], p2[:, 0:TT].rearrange("p (s l) -> p s l", s=4),
                     a1[:, 0:TT].rearrange("p (s l) -> p s l", s=4), ALU.mult)
                P.tt("dve", uprev[:, c, :], p2[:, TT:TT + 2], a1[:, TT:TT + 2], ALU.mult)
                P.copy("dve", convs_st[:, :, c, :], u3[:, :, 64:66])
                uv = lambda k: u3[:, :, k:k + 64]
                cvv = cvb[:, 0:TT].rearrange("p (s l) -> p s l", s=4)
            else:
                P.copy("dve", ub[:, 0:2], uprev[:, c, :])
                P.tt("dve", ub[:, 2:2 + TT], p2[:, 0:TT], a1[:, 0:TT], ALU.mult)
                P.copy("dve", uprev[:, c, :], ub[:, TT:TT + 2])
                uv = lambda k: ub[:, k:k + TT]
                cvv = cvb[:, 0:TT]
            cw = lambda k: cst[:, C_CONVW + c * 3 + k:C_CONVW + c * 3 + k + 1]
            P.ts("dve", cvv, uv(0), cw(0), None, ALU.mult)
            P.stt("dve", cvv, uv(1), cw(1), cvv, ALU.mult, ALU.add)
            P.stt("dve", cvv, uv(2), cw(2), cvv, ALU.mult, ALU.add)
            P.act(sgb[:, 0:TT], p4[:, 0:TT], AF.Silu)
            P.tt("dve", bsb[:, 0:TT], p3[:, 0:TT], sgb[:, 0:TT], ALU.mult)
            P.tt("dve", zA[:, c, :], cvb[:, 0:TT], bsb[:, 0:TT], ALU.mult)

    def gla(NT, sample, seqs=None):
        w, _ = wload(win_glr, None, KC, 16)
        pg = bank()
        fm(pg[0:16, 0:NT], w, 0, hT, NT, m=16)
        P.copy("act", glr[0:16, 0:NT], pg[0:16, 0:NT])
        if sample:
            chunks = [(s_ * 128 + j * 64, 64) for s_ in range(NS) for j in range(2)]
            Umat = UB64
        else:
            chunks = [(s_ * 128, 128) for s_ in range(NS)]
            Umat = U128

        def gate1(h):
            g = h % 2
            for s_ in range(NS):
                pxg = bank()
                P.mm(pxg[:, 0:DK], glr[0:17, s_ * 128:(s_ + 1) * 128], wgb[0:17, h * DK:(h + 1) * DK])
                P.act(ex[g][:, s_, :], pxg[:, 0:DK], AF.Exp, scale=-1.0)
                P.act(nla[g][:, s_, :], ex[g][:, s_, :], AF.Ln, bias=cst[:, C_ONE:C_ONE + 1])

        def gate2(h):
            g = h % 2
            for dc in range(2):
                pc = bank()
                for s_ in range(NS):
                    P.mm(pc[:, s_ * 128:(s_ + 1) * 128], nla[g][:, s_, dc * 128:(dc + 1) * 128], Umat)
                P.act(Ep[g][:, dc, :], pc[:, 0:TT], AF.Exp)
                P.act(En[g][:, dc, :], pc[:, 0:TT], AF.Exp, scale=-1.0)
                for ci, (c0, cn) in enumerate(chunks):
                    P.copy("dve", cl[g][:, dc, ci:ci + 1], pc[:, c0 + cn - 1:c0 + cn])
                for ci, (c0, cn) in enumerate(chunks):
                    P.act(Ee[g][:, dc, c0:c0 + cn], pc[:, c0:c0 + cn], AF.Exp, bias=cl[g][:, dc, ci:ci + 1], scale=-1.0)

        pending = []

        def flush():
            while pending:
                zz, h_, cols_ = pending.pop(0)
                pz = bank()
                pzb = pz[:].bitcast(BF16)
                for vc in range(4):
                    P.tr(pzb[:, vc * 128:(vc + 1) * 128], zz[:, vc * 128:(vc + 1) * 128], identb[:])
                P.copy("act", zB[:, h_ * 4:(h_ + 1) * 4, cols_], pzb[:, 0:512].rearrange("p (v l) -> p v l", v=4))

        gate1(0)
        gate2(0)
        for h in range(NH):
            g = h % 2
            if h + 1 < NH:
                gate1(h + 1)
            wqk, _ = wload(win_blk, 16 + 3 * h + 0)
            for dc in range(2):
                pq = bank()
                fm(pq[:, 0:NT], wqk, dc * 128, hT, NT)
                P.stt("dve", qf[:, dc, :], pq[:, 0:TT], DK ** -0.5, Ep[g][:, dc, :], ALU.mult, ALU.mult)
                pk = bank()
                fm(pk[:, 0:NT], wqk, 256 + dc * 128, hT, NT)
                P.tt("dve", kf[:, dc, :], pk[:, 0:TT], En[g][:, dc, :], ALU.mult)
                P.tt("dve", ke[:, dc, :], pk[:, 0:TT], Ee[g][:, dc, :], ALU.mult)
            if h + 1 < NH:
                gate2(h + 1)
            if sample:
                for j in range(2):
                    src = qf[:].rearrange("p d (s j l) -> p d s j l", s=NS, j=2)[:, :, :, j, :]
                    dst = qfz[j][:].rearrange("p d (s j l) -> p d s j l", s=NS, j=2)[:, :, :, j, :]
                    for dc in range(2):
                        P.copy("dve", dst[:, dc], src[:, dc])
            wv, _ = wload(win_blk, 16 + 3 * h + 1)
            for s_ in range(NS):
                pv = bank()
                tm(pv[:, 0:DV], wv, hT, s_)
                P.copy("act", vTM[:, s_, :], pv[:, 0:DV])
            ptr = bank()
            ptrb = ptr[:].bitcast(BF16)
            for s_ in range(NS):
                for dc in range(2):
                    P.tr(ptrb[:, (s_ * 2 + dc) * 128:(s_ * 2 + dc + 1) * 128], ke[:, dc, s_ * 128:(s_ + 1) * 128], identb[:])
            P.copy("act", keTM[:].rearrange("p s d -> p (s d)"), ptrb[:, 0:NS * DK])
            wg, _ = wload(win_blk, 16 + 3 * h + 2)
            for s_ in range(NS):
                pgg = bank()
                tm(pgg[:, 0:DV], wg, hT, s_)
                P.act(gsil[:, s_, :], pgg[:, 0:DV], AF.Silu)
                P.tt("dve", gsil[:, s_, :], gsil[:, s_, :], gnorm_b, ALU.mult)
            for s_ in range(NS):
                cols = slice(s_ * 128, (s_ + 1) * 128)
                if sample:
                    f0, b0, sn = S0f[0], S0b[0], alt(Snew, "snew")
                    for j in range(2):
                        seq = seqs[s_ * 2 + j]
                        src = sg_d[seq, h].rearrange("(dc p) v -> p dc v", p=128)
                        P.dma("sp", f0[:, j], src)
                        P.dma("pool", b0[:, j], src)
                pa = bank()
                for dc in range(2):
                    P.mm(pa[:, 0:128], kf[:, dc, cols], qf[:, dc, cols], start=(dc == 0), stop=(dc == 1))
                am = alt(attm, "attm")
                P.tt("dve", am[:], pa[:, 0:128], MB64 if sample else M128, ALU.mult)
                po = bank()
                if sample:
                    n = 0
                    for j in range(2):
                        for dc in range(2):
                            P.mm(po[:, 0:DV], qfz[j][:, dc, cols], b0[:, j, dc, :], start=(n == 0), stop=False)
                            n += 1
                else:
                    for dc in range(2):
                        P.mm(po[:, 0:DV], qf[:, dc, cols], Sbf[:, h * 2 + dc, :], start=(dc == 0), stop=False)
                pps = []
                if sample:
                    for j in range(2):
                        rows = slice(j * 64, (j + 1) * 64)
                        for dc in range(2):
                            pp = bank()
                            P.mm(pp[:, 0:DV], keTM[rows, s_, dc * 128:(dc + 1) * 128], vTM[rows, s_, :])
                            pps.append((j, dc, pp))
                else:
                    for dc in range(2):
                        pp = bank()
                        P.mm(pp[:, 0:DV], keTM[:, s_, dc * 128:(dc + 1) * 128], vTM[:, s_, :])
                        pps.append((0, dc, pp))
                P.mm(po[:, 0:DV], am[:], vTM[:, s_, :], start=False, stop=True)
                flush()
                if sample:
                    for (j, dc, pp) in pps:
                        ccol = s_ * 128 + j * 64 + 63
                        P.stt("dve", sn[:, j, dc, :], f0[:, j, dc, :], Ep[g][:, dc, ccol:ccol + 1], pp[:, 0:DV],
                              ALU.mult, ALU.add)
                    for j in range(2):
                        seq = seqs[s_ * 2 + j]
                        P.dma("sp", glas_o[seq, h].rearrange("(dc p) v -> p dc v", p=128), sn[:, j], out_dma=True)
                else:
                    for (j, dc, pp) in pps:
                        dec = Ep[g][:, dc, s_ * 128 + 127:s_ * 128 + 128]
                        P.stt("dve", Sst[:, h * 2 + dc, :], Sst[:, h * 2 + dc, :], dec, pp[:, 0:DV], ALU.mult, ALU.add)
                        P.copy("act", Sbf[:, h * 2 + dc, :], Sst[:, h * 2 + dc, :])
                zz = alt(zb, "zb")
                col = (h * NS + s_) % 8
                P.act(zz[:], po[:, 0:DV], AF.Square, scale=float(DV ** -0.5), accum=ssq[:, col:col + 1])
                rsqrt(rs[:, col:col + 1], ssq[:, col:col + 1])
                P.stt("dve", zz[:], po[:, 0:DV], rs[:, col:col + 1], gsil[:, s_, :], ALU.mult, ALU.mult)
                pending.append((zz, h, cols))
        flush()

    def xattn_setup(NT):
        w, _ = wload(win_blk, 28)
        for h in range(XH):
            pq = bank()
            fm(pq[:, 0:NT], w, h * 128, hT, NT)
            P.op("act", lambda e, o=qxT[:, h, :], i=pq[:, 0:TT]: e.mul(o, i, float(128 ** -0.5)),
                 reads=[pq[:, 0:TT]], writes=[qxT[:, h, :]])
        w, _ = wload(win_blk, 29)
        for h in range(XH):
            pg = bank()
            fm(pg[:, 0:NT], w, h * 128, hT, NT)
            P.act(sgx[:, h, :], pg[:, 0:TT], AF.Silu)

    def xattn_units(sample, seqs=None):
        if sample:
            units = [(i * 64, 64, i) for i in range(4)]
        else:
            units = [(s * 128, 128, None) for s in range(NS)]
        for (c0, nt, si) in units:
            if sample:
                KT, V = KTs[0], Vs[0]
                P.dma("pool", KT[:], mkc_d[seqs[si]])
                P.dma("pool", V[:], mvc_d[seqs[si]].rearrange("(mc p) n -> p mc n", p=128))
            else:
                KT, V = KTp, Vp
            pox = LB
            pscs = []
            for h in range(XH):
                psc = bank()
                P.mm(psc[0:nt, 0:NMEM], qxT[:, h, c0:c0 + nt], KT[:, h, :])
                pscs.append(psc)
            pnbs = []
            for h in range(XH):
                psc = pscs[h]
                col = h
                P.op("dve", lambda e, o=mx[0:nt, col:col + 1], i=psc[0:nt, 0:NMEM]: e.reduce_max(o, i, AX.X),
                     reads=[psc[0:nt, 0:NMEM]], writes=[mx[0:nt, col:col + 1]])
                P.ts("dve", mx[0:nt, col:col + 1], mx[0:nt, col:col + 1], -1.0, None, ALU.mult)
                pb = pe_[h]
                P.act(pb[0:nt, :], psc[0:nt, 0:NMEM], AF.Exp, bias=mx[0:nt, col:col + 1], accum=ssum[0:nt, col:col + 1])
                P.op("dve", lambda e, o=ssum[0:nt, col:col + 1]: e.reciprocal(o, o),
                     reads=[ssum[0:nt, col:col + 1]], writes=[ssum[0:nt, col:col + 1]])
                pnb = pn[h]
                P.ts("dve", pnb[0:nt, :], pb[0:nt, :], ssum[0:nt, col:col + 1], None, ALU.mult)
                pnbs.append(pnb)
            yield
            ptts = []
            for h in range(XH):
                pnb = pnbs[h]
                ptb = bank()
                ptbb = ptb[:].bitcast(BF16)
                for mc in range(2):
                    P.tr(ptbb[:, mc * 128:mc * 128 + nt], pnb[0:nt, mc * 128:(mc + 1) * 128], identb[0:nt, 0:nt])
                ptt = pT[h]
                for mc in range(2):
                    P.copy("act", ptt[:, mc, 0:nt], ptbb[:, mc * 128:mc * 128 + nt])
                ptts.append(ptt)
            for h in range(XH):
                ptt = ptts[h]
                for mc in range(2):
                    P.mm(pox[:, h * 128:h * 128 + nt], V[:, mc, h * 128:(h + 1) * 128], ptt[:, mc, 0:nt],
                         start=(mc == 0), stop=(mc == 1))
            for h in range(XH):
                P.tt("dve", zX[:, h, c0:c0 + nt], pox[:, h * 128:h * 128 + nt], sgx[:, h, c0:c0 + nt], ALU.mult)
            yield

    def merge_branch(br, NT):
        zsrc = (zA, zB, zX)[br]
        wo_d = (wouta, woutb, woutx)[br]
        kcn = (KC, KC, 4)[br]
        for g in range(4):
            wm, _ = wload(win_blk, 30 + br * 4 + g)
            s3s = []
            for oc in range(4):
                pm = bank()
                fm(pm[:, 0:NT], wm, oc * 128, hT, NT)
                s3 = sg3[oc]
                P.act(s3[:], pm[:, 0:TT], AF.Sigmoid)
                s3s.append(s3)
            wo, _ = wload(wo_d, g, kcn, 512)
            for oc in range(4):
                o = g * 4 + oc
                py = bank()
                for kc in range(kcn):
                    P.mm(py[:, 0:TT], wo[:, kc, oc * 128:(oc + 1) * 128], zsrc[:, kc, :],
                         start=(kc == 0), stop=(kc == kcn - 1))
                s3 = s3s[oc]
                if br == 0:
                    P.tt("dve", mixf[:, o, :], py[:, 0:TT], s3[:], ALU.mult)
                else:
                    tb = alt(t3, "t3")
                    P.tt("dve", tb[:], py[:, 0:TT], s3[:], ALU.mult)
                    if br == 1:
                        P.tt("dve", mixf[:, o, :], mixf[:, o, :], tb[:], ALU.add)
                    else:
                        P.tt("dve", mixb[:, o, :], mixf[:, o, :], tb[:], ALU.add)
            yield

    def run(gen):
        for _ in gen:
            pass

    def interleave(g1, g2):
        a1 = a2 = True
        while a1 or a2:
            if a1:
                try:
                    next(g1)
                except StopIteration:
                    a1 = False
            if a2:
                try:
                    next(g2)
                except StopIteration:
                    a2 = False

    def final(xi, yT_ap):
        pss = LB
        prev = None
        for g in range(4):
            w, _ = wload(wfin, g)
            for oc in range(4):
                o = g * 4 + oc
                py = bank()
                for kc in range(KC):
                    P.mm(py[:, 0:TT], w[:, kc, oc * 128:(oc + 1) * 128], mixb[:, kc, :],
                         start=(kc == 0), stop=(kc == KC - 1))
                if prev is not None:
                    P.mm(pss[:, 0:TT], onesb[:], prev[0][:, 0:TT], start=(prev[1] == 0), stop=False)
                s_ = alt(sq, "sq")
                P.act(s_[:, 0:TT], py[:, 0:TT], AF.Square)
                P.copy("act", yT[:, o, :], py[:, 0:TT])
                prev = (s_, o)
        P.mm(pss[:, 0:TT], onesb[:], prev[0][:, 0:TT], start=False, stop=True)
        rsqrt(rstd_b[:, 0:TT], pss[:, 0:TT])
        yv = yT_ap.rearrange("(kc p) n -> kc p n", p=128)
        for o in range(KC):
            ot = alt(outT, "outT")
            P.stt("dve", ot[:], yT[:, o, :], cst[:, C_GPOST + o:C_GPOST + o + 1], rstd_b[:, 0:TT], ALU.mult, ALU.mult)
            P.tt("dve", ot[:], ot[:], xinb[xi][:, o, 0:TT], ALU.add)
            P.dma("pool", yv[o], ot[:], out_dma=True)

    if upto >= 0.5:
        stage0(memT, NMEM, C_GMEM)
    if upto >= 1:
        w, _ = wload(wmk)
        for h in range(XH):
            pk = bank()
            fm(pk[:, 0:NMEM], w, h * 128, hT, NMEM)
            P.copy("act", KTp[:, h, :], pk[:, 0:NMEM])
            P.copy("dve", mkst[:, h, :], pk[:, 0:NMEM])
        P.dma("sp", mkT_o.rearrange("(h p) m -> p h m", p=128), mkst[:], out_dma=True)
        w, _ = wload(wmv)
        for mc in range(2):
            pv = bank()
            tm(pv[:, 0:512], w, hT, mc)
            P.copy("act", Vp[:, mc, :], pv[:, 0:512])
            P.copy("dve", mvst[:, mc, :], pv[:, 0:512])
        P.dma("sp", mv_o.rearrange("(mc p) n -> p mc n", p=128), mvst[:], out_dma=True)

    def xloadP(xT_ap):
        P.dma("pool", xinP[:], xT_ap.rearrange("(kc p) n -> p kc n", p=128))

    def stage0P(hTP):
        ps = LB
        for kc in range(KC):
            s_ = alt(sqP, "sqP")
            P.act(s_[:], xinP[:, kc, :], AF.Square)
            P.mm(ps[:, 0:TTP], onesb[:], s_[:], start=(kc == 0), stop=(kc == KC - 1))
        rsqrt(rstdP[:], ps[:, 0:TTP])
        for kc in range(KC):
            P.stt("dve", hTP[:, kc, :], xinP[:, kc, :], cst[:, C_GPRE + kc:C_GPRE + kc + 1],
                  rstdP[:], ALU.mult, ALU.mult)

    def glaP(hTP, mid_hook):
        w, _ = wload(win_glr, None, KC, 16)
        pg = bank()
        fm(pg[0:16, 0:TTP], w, 0, hTP, TTP, m=16)
        P.copy("act", glrP[0:16, :], pg[0:16, 0:TTP])

        def gate1(h):
            g = h % 2
            for s_ in range(NSP):
                pxg = bank()
                P.mm(pxg[:, 0:DK], glrP[0:17, s_ * 128:(s_ + 1) * 128], wgb[0:17, h * DK:(h + 1) * DK])
                P.act(exP[g][:, s_, :], pxg[:, 0:DK], AF.Exp, scale=-1.0)
                P.act(nlaP[g][:, s_, :], exP[g][:, s_, :], AF.Ln, bias=cst[:, C_ONE:C_ONE + 1])

        def gate2(h):
            g = h % 2
            for dc in range(2):
                pc = bank()
                for s_ in range(NSP):
                    P.mm(pc[:, s_ * 128:(s_ + 1) * 128], nlaP[g][:, s_, dc * 128:(dc + 1) * 128], U128)
                P.copy("dve", clP[g][:, dc, :], pc[:, 0:TTP].rearrange("p (s l) -> p s l", l=128)[:, :, 127])
                P.copy("dve", TsP[g][:, dc, NSP - 1:NSP], clP[g][:, dc, NSP - 1:NSP])
                for s_ in range(NSP - 2, -1, -1):
                    P.tt("dve", TsP[g][:, dc, s_:s_ + 1], TsP[g][:, dc, s_ + 1:s_ + 2], clP[g][:, dc, s_:s_ + 1], ALU.add)
                for s_ in range(NSP):
                    P.act(EeP[g][:, dc, s_ * 128:(s_ + 1) * 128], pc[:, s_ * 128:(s_ + 1) * 128], AF.Exp,
                          bias=TsP[g][:, dc, s_:s_ + 1], scale=-1.0)
                P.act(decP[g][:, dc, 0:1], TsP[g][:, dc, 0:1], AF.Exp)

        gate1(0)
        gate2(0)
        for h in range(NH):
            g = h % 2
            if h + 1 < NH:
                gate1(h + 1)
            wk, kc0 = wload(win_blk, 16 + 3 * h + 0, KC, 512, 256, 256)
            for dc in range(2):
                pk = bank()
                fm(pk[:, 0:TTP], wk, kc0 + dc * 128, hTP, TTP)
                P.tt("dve", keP[:, dc, :], pk[:, 0:TTP], EeP[g][:, dc, :], ALU.mult)
            if h + 1 < NH:
                gate2(h + 1)
            wv, _ = wload(win_blk, 16 + 3 * h + 1)
            for s_ in range(NSP):
                pv = bank()
                tm(pv[:, 0:DV], wv, hTP, s_)
                P.copy("act", vTMP[:, s_, :], pv[:, 0:DV])
            ptr = bank()
            ptrb = ptr[:].bitcast(BF16)
            for s_ in range(NSP):
                for dc in range(2):
                    P.tr(ptrb[:, (s_ * 2 + dc) * 128:(s_ * 2 + dc + 1) * 128], keP[:, dc, s_ * 128:(s_ + 1) * 128], identb[:])
            P.copy("act", keTMP[:].rearrange("p s d -> p (s d)"), ptrb[:, 0:NSP * DK])
            if h == 1:
                mid_hook()
            for dc in range(2):
                pp = bank()
                for s_ in range(NSP):
                    P.mm(pp[:, 0:DV], keTMP[:, s_, dc * 128:(dc + 1) * 128], vTMP[:, s_, :],
                         start=(s_ == 0), stop=(s_ == NSP - 1))
                P.stt("dve", Sst[:, h * 2 + dc, :], Sst[:, h * 2 + dc, :], decP[g][:, dc, 0:1], pp[:, 0:DV],
                      ALU.mult, ALU.add)

    if phase1:
        P.memset("dve", glrP[:], 1.0)
        NPT = 3 * SEG // TTP
        pcl = [(win_blk, c) for c in range(KC)]
        for g_ in range(4):
            pcl += [(win_blk, 30 + g_), (wouta, g_)]
        pcl += [(win_blk, 16 + 3 * h_ + 2) for h_ in range(NH)]
        pcl += [(win_blk, 28), (win_blk, 29)]
        for g_ in range(4):
            pcl += [(win_blk, 34 + g_), (woutb, g_)]
        for g_ in range(4):
            pcl += [(win_blk, 38 + g_), (woutx, g_)]
        pcl += [(wfin, g_) for g_ in range(4)]
        xloadP(xpre[:, 0:TTP])
        stage0P(hTPb[0])
        for t in range(NPT):
            def hook(t=t):
                if t + 1 < NPT:
                    stage0P(hTPb[(t + 1) % 2])
            if t + 1 < NPT:
                xloadP(xpre[:, (t + 1) * TTP:(t + 2) * TTP])
            for _ in range(3):
                if pcl:
                    precast(*pcl.pop(0))
            glaP(hTPb[t % 2], hook)
        while pcl:
            precast(*pcl.pop(0))
        P.copy("act", Sbf[:].rearrange("p a v -> p (a v)"), Sst[:].rearrange("p a v -> p (a v)"))
    P.memset("dve", qfz[0][:], 0.0)
    P.memset("dve", qfz[1][:], 0.0)

    seqs = [0, 1, 2, 3]
    if upto >= 2:
        stage0(xsT, TT + NHALO, C_GPRE)
    if upto >= 3:
        branch_a(TT + NHALO, True)
    if upto >= 4:
        run(merge_branch(0, TT + NHALO))
    if upto >= 5:
        gla(TT + NHALO, True, seqs=seqs)
    if upto >= 8:
        xattn_setup(TT + NHALO)
        interleave(xattn_units(True, seqs), merge_branch(1, TT + NHALO))
        run(merge_branch(2, TT + NHALO))
    if upto >= 9:
        final(0, ysT)
        P.dma("sp", convs_o[:, :, :, :], convs_st[:], out_dma=True)
    if upto < 10:
        NTP = 0

    if NTP > 0:
        xload(xpT[:, 0:TT], TT, 1)
    for t in range(NTP):
        xi = (t + 1) % 2
        xs_ = xpT[:, t * TT:(t + 1) * TT]
        stage0(xs_, TT, C_GPRE, load=False, xi=xi)
        branch_a(TT, False)
        run(merge_branch(0, TT))
        gla(TT, False)
        xattn_setup(TT)
        interleave(xattn_units(False), merge_branch(1, TT))
        run(merge_branch(2, TT))
        if t + 1 < NTP:
            xload(xpT[:, (t + 1) * TT:(t + 2) * TT], TT, 1 - xi)
        final(xi, ypT[:, t * TT:(t + 1) * TT])
    P.dma("sp", convp_o[:, :, :], uprev[:], out_dma=True)
    P.dma("sp", glap_o.rearrange("h (dc p) v -> p (h dc) v", p=128), Sst[:], out_dma=True)

    P.emit(st)
    st.close()
    return nc


def _tile_w(w, ncol=512):
    K, N = w.shape
    return np.ascontiguousarray(w.reshape(K // 128, 128, N // ncol, ncol).transpose(2, 1, 0, 3))


def _fm_vec(v):
    return np.ascontiguousarray(v.reshape(KC, 128).T)


def _const_pack(g_pre, g_post, g_mem, conv_w, gla_norm, core):
    c = np.zeros((128, CST_N), np.float32)
    i = np.arange(128)
    c[:, C_IDENT:C_IDENT + 128] = np.eye(128, dtype=np.float32)
    up = (i[:, None] <= i[None, :]).astype(np.float32)
    blk = ((i[:, None] // 64) == (i[None, :] // 64)).astype(np.float32)
    c[:, C_U128:C_U128 + 128] = up * (-1.0 / GATE_TAU)
    c[:, C_UB64:C_UB64 + 128] = up * blk * (-1.0 / GATE_TAU)
    c[:, C_M128:C_M128 + 128] = up
    c[:, C_MB64:C_MB64 + 128] = up * blk
    c[:, C_ONES:C_ONES + 128] = 1.0 / D
    c[:, C_GPRE:C_GPRE + 16] = _fm_vec(g_pre)
    c[:, C_GPOST:C_GPOST + 16] = _fm_vec(g_post)
    c[:, C_GMEM:C_GMEM + 16] = _fm_vec(g_mem)
    c[:, C_CONVW:C_CONVW + 48] = conv_w.reshape(3, KC, 128).transpose(2, 1, 0).reshape(128, 48)
    c[:, C_GNORM:C_GNORM + 512] = gla_norm[None, :]
    c[:, C_EPS] = EPS
    c[:, C_ONE] = 1.0
    seg = core % 4
    base = core - seg
    for ii in range(NCORE):
        if base <= ii < core:
            c[:, C_MASK + ii] = 1.0
            for m in range(ii + 1, core):
                c[:, C_CMAT + ii * 8 + m] = 1.0
    return c


_CACHE = {}


def kernel(x_prompt, x_sample, cache_conv, state_gla, cache_mem_k, cache_mem_v, mem_prompt,
           g_pre, g_post, g_mem, w_in, conv_w, w_gate_up, b_gate, gla_norm,
           w_mem_k, w_mem_v, w_out_a, w_out_b, w_out_x, w_final, _phase1=True, _ntiles=None, _upto=99):
    f = lambda a: np.asarray(a, dtype=np.float32)
    x_prompt, x_sample, cache_conv, state_gla = f(x_prompt), f(x_sample), f(cache_conv), f(state_gla)
    cache_mem_k, cache_mem_v, mem_prompt = f(cache_mem_k), f(cache_mem_v), f(mem_prompt)
    B, SEQ, _ = x_prompt.shape
    SEG = SEQ // 4
    key = (SEG, _phase1, _ntiles, _upto)
    if key not in _CACHE:
        _CACHE[key] = build_program(SEG, phase1=_phase1, n_prompt_tiles=_ntiles, upto=_upto)
    nc = _CACHE[key]

    w_in0 = f(w_in)[0]
    o_ain, o_ab, o_ac, o_ag = 0, 2048, 4096, 6144
    o_q, o_k, o_v, o_lr, o_gg, o_xq, o_xg, o_m = 8192, 9216, 10240, 12288, 12304, 14352, 14864, 15376
    cols = []
    for c in range(KC):
        for o in (o_ain, o_ac, o_ab, o_ag):
            cols.append(np.arange(o + c * 128, o + (c + 1) * 128))
    for h in range(NH):
        cols.append(np.arange(o_q + h * DK, o_q + (h + 1) * DK))
        cols.append(np.arange(o_k + h * DK, o_k + (h + 1) * DK))
        cols.append(np.arange(o_v + h * DV, o_v + (h + 1) * DV))
        cols.append(np.arange(o_gg + h * DV, o_gg + (h + 1) * DV))
    cols.append(np.arange(o_xq, o_xq + 512))
    cols.append(np.arange(o_xg, o_xg + 512))
    cols.append(np.arange(o_m, o_m + 3 * D))
    perm = np.concatenate(cols)
    assert perm.size == 42 * 512
    win_blk = _tile_w(w_in0[:, perm])
    win_glr = np.ascontiguousarray(w_in0[:, o_lr:o_lr + 16].reshape(KC, 128, 16).transpose(1, 0, 2))
    wouta = _tile_w(f(w_out_a)[0])
    woutb = _tile_w(f(w_out_b)[0])
    woutx = _tile_w(f(w_out_x)[0])
    wfin = _tile_w(f(w_final)[0])
    wmk = _tile_w(f(w_mem_k)[0])[0]
    wmv = _tile_w(f(w_mem_v)[0])[0]
    wgb = np.zeros((32, 1024), np.float32)
    wgb[0:16] = f(w_gate_up)[0]
    wgb[16] = f(b_gate)[0]

    in_maps = []
    for core in range(NCORE):
        b, seg = core // 4, core % 4
        xp = x_prompt[b, seg * SEG:(seg + 1) * SEG]
        halo = x_prompt[b, seg * SEG - 2:seg * SEG] if seg > 0 else np.zeros((2, D), np.float32)
        xs = np.concatenate([x_sample[4 * core:4 * core + 4].reshape(TT, D), halo], axis=0)
        pre = np.zeros((D, 3 * SEG), np.float32)
        if seg > 0:
            pre[:, (3 - seg) * SEG:] = x_prompt[b, 0:seg * SEG].T
        in_maps.append({
            "xpre": pre,
            "xpT": np.ascontiguousarray(xp.T),
            "xsT": np.ascontiguousarray(xs.T),
            "memT": np.ascontiguousarray(mem_prompt[b].T),
            "cst": _const_pack(f(g_pre)[0], f(g_post)[0], f(g_mem)[0], f(conv_w)[0], f(gla_norm)[0], core),
            "wgb": wgb,
            "win_blk": win_blk, "win_glr": win_glr, "wouta": wouta, "woutb": woutb, "woutx": woutx,
            "wfin": wfin, "wmk": wmk, "wmv": wmv,
            "convc": np.ascontiguousarray(
                cache_conv[0, 4 * core:4 * core + 4].reshape(4, 2, KC, 128).transpose(3, 0, 2, 1)),
            "sg": np.ascontiguousarray(state_gla[0, 4 * core:4 * core + 4]),
            "mkc": np.ascontiguousarray(cache_mem_k[0, 4 * core:4 * core + 4].transpose(0, 3, 2, 1)),
            "mvc": np.ascontiguousarray(cache_mem_v[0, 4 * core:4 * core + 4].reshape(4, NMEM, 512)),
        })
    if not _phase1:
        for m in in_maps:
            m.pop("xpre")
    res = run_bass_kernel_spmd(nc, in_maps, core_ids=list(range(NCORE)))
    R = res.results

    y_prompt = np.empty((B, SEQ, D), np.float32)
    y_sample = np.empty((32, 64, D), np.float32)
    conv_s = np.empty((1, 32, 2, D), np.float32)
    gla_s = np.empty((1, 32, NH, DK, DV), np.float32)
    for core in range(NCORE):
        b, seg = core // 4, core % 4
        y_prompt[b, seg * SEG:(seg + 1) * SEG] = R[core]["ypT"].T
        y_sample[4 * core:4 * core + 4] = R[core]["ysT"].T.reshape(4, 64, D)
        conv_s[0, 4 * core:4 * core + 4] = R[core]["convs"].transpose(1, 3, 2, 0).reshape(4, 2, D)
        gla_s[0, 4 * core:4 * core + 4] = R[core]["glas"]
    conv_p = np.stack([R[c]["convp"].transpose(2, 1, 0).reshape(2, D) for c in (3, 7)])[None]
    gla_p = np.stack([R[c]["glap"] for c in (3, 7)])[None]
    mem_k = np.stack([R[c]["mkT"].T.reshape(NMEM, XH, 128) for c in (0, 4)])[None]
    mem_v = np.stack([R[c]["mvo"].reshape(NMEM, XH, 128) for c in (0, 4)])[None]
    return (y_prompt, y_sample, conv_p.astype(np.float32), gla_p.astype(np.float32),
            mem_k.astype(np.float32), mem_v.astype(np.float32), conv_s, gla_s)
```
